# Optimizing a Trainium2 kernel written in Bass

```python
import jax
import jax.numpy as jnp
from jax import lax
import numpy as np


D_MODEL = 1024
BATCH = 16
SEQ = 2048
DEPTH = 4

N_MIXERS = 2
RET_HEADS = 4
RET_QK_DIM = D_MODEL // RET_HEADS
RET_V_DIM = 2 * RET_QK_DIM
RET_CHUNK = 128
RET_IN_DIM = 2 * RET_HEADS * RET_QK_DIM + 2 * RET_HEADS * RET_V_DIM
NSA_HEADS = 16
NSA_GROUPS = 4
NSA_HPG = NSA_HEADS // NSA_GROUPS
NSA_HEAD_DIM = D_MODEL // NSA_HEADS
NSA_KV_DIM = NSA_GROUPS * NSA_HEAD_DIM
NSA_IN_DIM = D_MODEL + 6 * NSA_KV_DIM + 3 * NSA_HEADS
CMP_BLOCK = 32
CMP_STRIDE = 16
CMP_HIDDEN = 256
SEL_BLOCK = 64
SEL_TOP = 16
WINDOW = 512
Q_BLOCK = 128
FORCED_SCORE = 1e4
INVALID_SCORE = -1.0
D_FF = 2816
FFN_RES = 0.5
DN_ALPHA = (2 * DEPTH) ** 0.25
DN_BETA = (8 * DEPTH) ** -0.25
LN_EPS = 1e-5
GN_EPS = 1e-6
NEG_INF = -1e30
ADA_GAIN = 0.1
N_RET_LAYERS = (DEPTH + 1) // 2
N_NSA_LAYERS = DEPTH // 2

kernel_name = 'hybrid_retention_nsa_macaron_deepnorm'


def layer_norm(x, g, b):
    xf = x.astype(jnp.float32)
    mu = jnp.mean(xf, -1, keepdims=True)
    var = jnp.mean(jnp.square(xf - mu), -1, keepdims=True)
    return ((xf - mu) * lax.rsqrt(var + LN_EPS) * g + b).astype(x.dtype)


def modulate(x, shift, scale):
    return x * (1.0 + scale[:, None, :]) + shift[:, None, :]


def post_norm_residual(x, y, gate, res_w, g, b):
    return layer_norm(DN_ALPHA * x + res_w * (1.0 + gate[:, None, :]) * y, g, b)


def swiglu(h, w_in, w_out):
    a, u = jnp.split(h @ w_in, 2, axis=-1)
    return (jax.nn.silu(a) * u) @ w_out


def masked_softmax(s, mask, axis):
    p = jax.nn.softmax(jnp.where(mask, s, NEG_INF), axis=axis)
    return p * mask


def retention(h, w_in, w_out):
    B, S, _ = h.shape
    C = RET_CHUNK
    n = S // C
    qk = RET_HEADS * RET_QK_DIM
    q, k, v, g = jnp.split(h @ w_in, [qk, 2 * qk, 2 * qk + RET_HEADS * RET_V_DIM], axis=-1)

    def heads(t, d):
        return t.reshape(B, n, C, RET_HEADS, d).transpose(1, 0, 3, 2, 4).astype(jnp.float32)

    qc = heads(q, RET_QK_DIM)
    kc = heads(k, RET_QK_DIM) * (RET_QK_DIM ** -0.5)
    vc = heads(v, RET_V_DIM)
    log_g = jnp.log(1.0 - jnp.exp2(-5.0 - jnp.arange(RET_HEADS, dtype=jnp.float32)))
    i = jnp.arange(C, dtype=jnp.float32)
    diff = i[:, None] - i[None, :]
    decay_intra = jnp.where(diff >= 0, jnp.exp(log_g[:, None, None] * jnp.maximum(diff, 0.0)), 0.0)
    decay_q = jnp.exp(log_g[:, None] * (i + 1.0))[:, :, None]
    decay_k = jnp.exp(log_g[:, None] * (C - 1.0 - i))[:, :, None]
    decay_state = jnp.exp(log_g * C)[:, None, None]

    def step(state, inp):
        qi, ki, vi = inp
        scores = jnp.einsum('bhcd,bhkd->bhck', qi, ki) * decay_intra
        o = jnp.einsum('bhck,bhke->bhce', scores, vi) + jnp.einsum('bhcd,bhde->bhce', qi * decay_q, state)
        state = decay_state * state + jnp.einsum('bhcd,bhce->bhde', ki * decay_k, vi)
        return state, o

    state0 = jnp.zeros((B, RET_HEADS, RET_QK_DIM, RET_V_DIM), jnp.float32)
    _, o = lax.scan(step, state0, (qc, kc, vc))
    mu = jnp.mean(o, -1, keepdims=True)
    var = jnp.mean(jnp.square(o - mu), -1, keepdims=True)
    o = (o - mu) * lax.rsqrt(var + GN_EPS)
    o = o.transpose(1, 0, 3, 2, 4).reshape(B, S, RET_HEADS * RET_V_DIM)
    return (o * jax.nn.silu(g.astype(jnp.float32))).astype(h.dtype) @ w_out


def alibi_slopes(n):
    return jnp.exp2(-8.0 * jnp.arange(1, n + 1, dtype=jnp.float32) / n)


def cmp_to_sel_overlap(n_cmp, n_sel):
    c0 = np.arange(n_cmp) * CMP_STRIDE
    s0 = np.arange(n_sel) * SEL_BLOCK
    lo = np.maximum(c0[:, None], s0[None, :])
    hi = np.minimum(c0[:, None] + CMP_BLOCK, s0[None, :] + SEL_BLOCK)
    return (np.clip(hi - lo, 0, None) / CMP_BLOCK).astype(np.float32)


def compress(t, pe, w1, w2):
    B, S, G, dh = t.shape
    n_cmp = (S - CMP_BLOCK) // CMP_STRIDE + 1
    idx = jnp.arange(n_cmp)[:, None] * CMP_STRIDE + jnp.arange(CMP_BLOCK)[None, :]
    blocks = t[:, idx] + pe[None, None, :, None, :]
    blocks = blocks.transpose(0, 1, 3, 2, 4).reshape(B, n_cmp, G, CMP_BLOCK * dh)
    return jax.nn.gelu(blocks @ w1) @ w2


def nsa(h, w_in, w_out, pe_k, pe_v, ck_w1, ck_w2, cv_w1, cv_w2):
    B, S, D = h.shape
    G, HPG, dh, kv = NSA_GROUPS, NSA_HPG, NSA_HEAD_DIM, NSA_KV_DIM
    splits = [D + j * kv for j in range(7)]
    q, kc, vc, ks, vs, kw, vw, gates = jnp.split(h @ w_in, splits, axis=-1)
    q = q.reshape(B, S, G, HPG, dh) * (dh ** -0.5)
    kc = compress(kc.reshape(B, S, G, dh), pe_k, ck_w1, ck_w2)
    vc = compress(vc.reshape(B, S, G, dh), pe_v, cv_w1, cv_w2)
    n_cmp = kc.shape[1]
    n_sel = S // SEL_BLOCK
    top = min(SEL_TOP, n_sel)
    ks = ks.reshape(B, n_sel, SEL_BLOCK, G, dh).transpose(0, 3, 1, 2, 4)
    vs = vs.reshape(B, n_sel, SEL_BLOCK, G, dh).transpose(0, 3, 1, 2, 4)
    pad = ((0, 0), (WINDOW, 0), (0, 0), (0, 0))
    kw = jnp.pad(kw.reshape(B, S, G, dh), pad)
    vw = jnp.pad(vw.reshape(B, S, G, dh), pad)
    gates = jax.nn.sigmoid(gates.astype(jnp.float32)).reshape(B, S, G, HPG, 3)
    slopes = alibi_slopes(NSA_HEADS).reshape(G, HPG)
    cmp_end = jnp.arange(n_cmp) * CMP_STRIDE + CMP_BLOCK - 1
    overlap = jnp.asarray(cmp_to_sel_overlap(n_cmp, n_sel))
    blk = jnp.arange(n_sel)
    n_qb = S // Q_BLOCK
    span = WINDOW + Q_BLOCK

    def block_fn(bq):
        b = bq // n_qb
        q0 = (bq % n_qb) * Q_BLOCK
        t = q0 + jnp.arange(Q_BLOCK)
        qi = lax.dynamic_slice(q, (b, q0, 0, 0, 0), (1, Q_BLOCK, G, HPG, dh))[0]
        kcb, vcb = kc[b], vc[b]
        dist_c = t[:, None] - cmp_end[None, :]
        s = jnp.einsum('qghd,ngd->ghqn', qi, kcb).astype(jnp.float32) - slopes[:, :, None, None] * dist_c.astype(jnp.float32)
        p_cmp = masked_softmax(s, dist_c >= 0, -1)
        o_cmp = jnp.einsum('ghqn,ngd->qghd', p_cmp, vcb)
        imp = jnp.einsum('ghqn,nj->gqj', p_cmp, overlap)
        cur = (t // SEL_BLOCK)[:, None]
        forced = (blk == 0) | (blk == cur) | (blk == cur - 1)
        valid = blk[None, :] * SEL_BLOCK <= t[:, None]
        score = jnp.where(valid, jnp.where(forced, FORCED_SCORE, imp), INVALID_SCORE)
        _, sel = lax.top_k(score, top)
        k_g = jax.vmap(lambda kk, ii: kk[ii])(ks[b], sel)
        v_g = jax.vmap(lambda vv, ii: vv[ii])(vs[b], sel)
        pos = sel[..., None] * SEL_BLOCK + jnp.arange(SEL_BLOCK)
        dist_s = t[None, :, None, None] - pos
        s = jnp.einsum('qghd,gqnkd->ghqnk', qi, k_g).astype(jnp.float32) - slopes[:, :, None, None, None] * dist_s[:, None].astype(jnp.float32)
        p = masked_softmax(s, (dist_s >= 0)[:, None], (-2, -1))
        o_sel = jnp.einsum('ghqnk,gqnkd->qghd', p, v_g)
        kwb = lax.dynamic_slice(kw, (b, q0, 0, 0), (1, span, G, dh))[0]
        vwb = lax.dynamic_slice(vw, (b, q0, 0, 0), (1, span, G, dh))[0]
        s_pos = q0 - WINDOW + jnp.arange(span)
        dist_w = t[:, None] - s_pos[None, :]
        mask_w = (dist_w >= 0) & (dist_w < WINDOW) & (s_pos >= 0)[None, :]
        s = jnp.einsum('qghd,kgd->ghqk', qi, kwb).astype(jnp.float32) - slopes[:, :, None, None] * dist_w.astype(jnp.float32)
        p = masked_softmax(s, mask_w, -1)
        o_win = jnp.einsum('ghqk,kgd->qghd', p, vwb)
        gb = lax.dynamic_slice(gates, (b, q0, 0, 0, 0), (1, Q_BLOCK, G, HPG, 3))[0]
        o = gb[..., 0:1] * o_cmp + gb[..., 1:2] * o_sel + gb[..., 2:3] * o_win
        return o.astype(h.dtype)

    o = lax.map(block_fn, jnp.arange(B * n_qb))
    return o.reshape(B, S, D) @ w_out


def setup_inputs(seed: int = 0) -> dict:
    key = jax.random.key(seed)
    ks = jax.random.split(key, 20)

    def w(k, shape, fan_in, gain=1.0):
        return jax.random.normal(k, shape, jnp.float32) * (gain * fan_in ** -0.5)

    D = D_MODEL
    dh = NSA_HEAD_DIM
    return {
        'x': jax.random.normal(ks[0], (BATCH, SEQ, D), jnp.float32),
        'c': jax.random.normal(ks[1], (BATCH, D), jnp.float32),
        'ada_w': w(ks[2], (DEPTH, D, 9 * D), D, ADA_GAIN),
        'ada_b': 0.01 * jax.random.normal(ks[3], (DEPTH, 9 * D), jnp.float32),
        'ln_g': 1.0 + 0.02 * jax.random.normal(ks[4], (DEPTH, 3, D), jnp.float32),
        'ln_b': 0.02 * jax.random.normal(ks[5], (DEPTH, 3, D), jnp.float32),
        'ffn_w_in': w(ks[6], (DEPTH, 2, D, 2 * D_FF), D),
        'ffn_w_out': w(ks[7], (DEPTH, 2, D_FF, D), D_FF, DN_BETA),
        'ret_w_in': w(ks[8], (N_RET_LAYERS, D, RET_IN_DIM), D),
        'ret_w_out': w(ks[9], (N_RET_LAYERS, RET_HEADS * RET_V_DIM, D), RET_HEADS * RET_V_DIM, DN_BETA),
        'nsa_w_in': w(ks[10], (N_NSA_LAYERS, D, NSA_IN_DIM), D),
        'nsa_w_out': w(ks[11], (N_NSA_LAYERS, D, D), D, DN_BETA),
        'nsa_pe_k': 0.1 * jax.random.normal(ks[12], (N_NSA_LAYERS, CMP_BLOCK, dh), jnp.float32),
        'nsa_pe_v': 0.1 * jax.random.normal(ks[13], (N_NSA_LAYERS, CMP_BLOCK, dh), jnp.float32),
        'nsa_ck_w1': w(ks[14], (N_NSA_LAYERS, CMP_BLOCK * dh, CMP_HIDDEN), CMP_BLOCK * dh),
        'nsa_ck_w2': w(ks[15], (N_NSA_LAYERS, CMP_HIDDEN, dh), CMP_HIDDEN),
        'nsa_cv_w1': w(ks[16], (N_NSA_LAYERS, CMP_BLOCK * dh, CMP_HIDDEN), CMP_BLOCK * dh),
        'nsa_cv_w2': w(ks[17], (N_NSA_LAYERS, CMP_HIDDEN, dh), CMP_HIDDEN),
    }


def reference(x, c, ada_w, ada_b, ln_g, ln_b, ffn_w_in, ffn_w_out, ret_w_in, ret_w_out,
              nsa_w_in, nsa_w_out, nsa_pe_k, nsa_pe_v, nsa_ck_w1, nsa_ck_w2, nsa_cv_w1, nsa_cv_w2):
    c_act = jax.nn.silu(c)
    for i in range(DEPTH):
        mod = c_act @ ada_w[i] + ada_b[i]
        sh0, sc0, g0, sh1, sc1, g1, sh2, sc2, g2 = jnp.split(mod, 9, axis=-1)
        y = swiglu(modulate(x, sh0, sc0), ffn_w_in[i, 0], ffn_w_out[i, 0])
        x = post_norm_residual(x, y, g0, FFN_RES, ln_g[i, 0], ln_b[i, 0])
        h = modulate(x, sh1, sc1)
        j = i // N_MIXERS
        if i % N_MIXERS == 0:
            y = retention(h, ret_w_in[j], ret_w_out[j])
        else:
            y = nsa(h, nsa_w_in[j], nsa_w_out[j], nsa_pe_k[j], nsa_pe_v[j],
                    nsa_ck_w1[j], nsa_ck_w2[j], nsa_cv_w1[j], nsa_cv_w2[j])
        x = post_norm_residual(x, y, g1, 1.0, ln_g[i, 1], ln_b[i, 1])
        y = swiglu(modulate(x, sh2, sc2), ffn_w_in[i, 1], ffn_w_out[i, 1])
        x = post_norm_residual(x, y, g2, FFN_RES, ln_g[i, 2], ln_b[i, 2])
    return x
```

```python
import numpy as np
import concourse.bass as bass
import concourse.mybir as mybir
from concourse.bass_utils import run_bass_kernel_spmd

F32 = mybir.dt.float32
BF16 = mybir.dt.bfloat16
AF = mybir.ActivationFunctionType
ALU = mybir.AluOpType
AX = mybir.AxisListType

D = 1024
SEQ = 2048
NB = 2
TOK = NB * SEQ
DEPTH = 4
DFF = 2816
NJ = DFF // 128
NT = 256
import os
NG = int(os.environ.get('NGRUN', TOK // NT))
ALPHA = (2 * DEPTH) ** 0.25
LN_EPS = 1e-5
EPS_P = LN_EPS / (ALPHA * ALPHA)
FFN_RES = 0.5

ENGS = ("pe", "act", "dve", "pool", "sp")


class Sem:
    def __init__(self, nc, stack, name):
        self.h = stack.enter_context(nc.semaphore(name))
        self.v = 0


class Ctx:
    def __init__(self, nc, stack):
        self.nc = nc
        self.stack = stack
        self.semstack = stack
        self.sems = {}
        self.eng = {"pe": nc.tensor, "act": nc.scalar, "dve": nc.vector, "pool": nc.gpsimd, "sp": nc.sync}
        self.esem = {e: Sem(nc, stack, "s_" + e) for e in ENGS if e != "sp"}
        self.waited = {e: {} for e in ENGS}
        self.ninst = 0
        self.nsig = 0
        self.strict = True
        self.last_ev = {}

    def sem(self, name):
        if name not in self.sems:
            self.sems[name] = Sem(self.nc, self.semstack, name)
        return self.sems[name]

    def _waits(self, eng, wait):
        out = []
        for ev in wait:
            if ev is None:
                continue
            s, v = ev
            if v <= 0:
                continue
            if self.waited[eng].get(id(s), 0) >= v:
                continue
            self.waited[eng][id(s)] = v
            out.append((s.h, v))
        return out

    def op(self, eng, fn, wait=(), sig=False):
        e = self.eng[eng]
        strict = self.strict and eng == "dve" and fn is not None
        if strict:
            wait = list(wait) + [self.last_ev.get(eng)]
            sig = True
        for (h, v) in self._waits(eng, wait):
            e.wait_ge(h, v)
        if fn is None:
            return None
        ins = fn(e)
        self.ninst += 1
        if sig:
            s = self.esem[eng]
            s.v += 1
            ins.then_inc(s.h, 1)
            self.nsig += 1
            self.last_ev[eng] = (s, s.v)
            return (s, s.v)
        return None

    def dma(self, eng, out, in_, sem, wait=()):
        e = self.eng[eng]
        for (h, v) in self._waits(eng, wait):
            e.wait_ge(h, v)
        sem.v += 16
        e.dma_start(out=out, in_=in_).then_inc(sem.h, 16)
        self.ninst += 1
        return (sem, sem.v)

    def emit(self):
        pass

    def barrier(self, evs):
        for e in ENGS:
            self.op(e, None, wait=evs)


_UNIQ = [0]
DBG_LAYOUT = []


def _uname(name):
    _UNIQ[0] += 1
    return "%s_u%d" % (name, _UNIQ[0])


def dbg_dump(ctx, G, items, wait):
    if "dbg" not in G:
        return None
    sem = ctx.sem("dbgsem")
    col = G.setdefault("dbg_col", [0])
    ev = None
    for (name, ap, n) in items:
        ev = ctx.dma("pool", G["dbg"][:, col[0]:col[0] + n], ap, sem, wait=wait)
        G["dbg_layout"].append((name, col[0], n))
        col[0] += n
    return ev


def sb(ctx, name, shape, dt):
    return ctx.stack.enter_context(ctx.nc.sbuf_tensor(_uname(name), shape, dt))


def ps(ctx, name, shape, dt=F32):
    return ctx.stack.enter_context(ctx.nc.psum_tensor(_uname(name), shape, dt))


def ada_phase(ctx, G, cT, ada_w, ada_bT):
    nc = ctx.nc
    PIECE = 1152
    NP = 9216 // PIECE
    with ctx.nc.sbuf_tensor("ada_wbuf", [128, 2, 8, PIECE], BF16) as wbuf, \
            ctx.nc.sbuf_tensor("ada_ctb", [128, 8, NB], BF16) as ctb, \
            ctx.nc.sbuf_tensor("ada_bT_sb", [128, DEPTH, 72], F32) as bT, \
            ctx.nc.psum_tensor("ada_ps0", [128, 72, 2], F32) as mps0, \
            ctx.nc.psum_tensor("ada_ps1", [128, 72, 2], F32) as mps1:
        mps = [mps0, mps1]
        ct = G["cT"]
        s_in = ctx.sem("adain")
        s_w = [ctx.sem("adaw0"), ctx.sem("adaw1")]
        ctx.dma("sp", bT[:], ada_bT, s_in)
        ev_in = ctx.dma("sp", ct[:], cT, s_in)
        ev_c = ctx.op("act", lambda e: e.activation(out=ctb[:], in_=ct[:], func=AF.Silu), wait=[ev_in], sig=True)
        free = [None, None]
        ev_copy = [None, None]
        n = 0
        for L in range(DEPTH):
            wv = ada_w[L].rearrange("(k p) n -> p k n", p=128)
            last = None
            for pc in range(NP):
                sl = n % 2
                for k in range(8):
                    ev_ld = ctx.dma("pool", wbuf[:, sl, k, :], wv[:, k, pc * PIECE:(pc + 1) * PIECE], s_w[sl], wait=[free[sl]])
                for cc in range(PIECE // 128):
                    ch = pc * (PIECE // 128) + cc
                    for k in range(8):
                        last = ctx.op("pe", lambda e, sl=sl, k=k, cc=cc, ch=ch, L=L: e.matmul(
                            mps[L % 2][:, ch, :], lhsT=wbuf[:, sl, k, cc * 128:(cc + 1) * 128], rhs=ctb[:, k, :],
                            start=(k == 0), stop=(k == 7)),
                            wait=[ev_ld, ev_c, ev_copy[L % 2]] if (cc == 0 and k == 0) else (),
                            sig=(cc == PIECE // 128 - 1 and k == 7))
                free[sl] = last
                n += 1
            for b in range(NB):
                ev_copy[L % 2] = ctx.op("dve", lambda e, L=L, b=b: e.tensor_tensor(
                    G["modT"][:, L, :, b], mps[L % 2][:, :, b], bT[:, L, :], op=ALU.add),
                    wait=[last, ev_in], sig=(b == NB - 1))
        last = None
        for L in range(DEPTH):
            for s in range(3):
                for b in range(NB):
                    sc = G["modT"][:, L, (3 * s + 1) * 8:(3 * s + 1) * 8 + 8, b]
                    gt = G["modT"][:, L, (3 * s + 2) * 8:(3 * s + 2) * 8 + 8, b]
                    rw = (1.0 if s == 1 else FFN_RES) / ALPHA
                    ctx.op("dve", lambda e, L=L, s=s, b=b, sc=sc: e.tensor_scalar_add(G["A"][:, L, s, b, :], sc, 1.0),
                           wait=[ev_copy[0], ev_copy[1]])
                    last = ctx.op("dve", lambda e, L=L, s=s, b=b, gt=gt, rw=rw: e.tensor_scalar(
                        G["C"][:, L, s, b, :], gt, 1.0, rw, op0=ALU.add, op1=ALU.mult), sig=(L == DEPTH - 1 and s == 2 and b == NB - 1))
        ctx.barrier([last])
    return last


class Epi:
    def __init__(self, ctx, G, zb=None, zsq=None, nt=None, Y=None, ST=None):
        nt = nt or NT
        self.nt = nt
        self.xbuf = [sb(ctx, "xbuf%d" % i, [128, 8, nt], F32) for i in range(2)]
        self.zb = zb if zb is not None else sb(ctx, "zb", [128, 8, nt], BF16)
        self.zsq = zsq if zsq is not None else sb(ctx, "zsq", [128, 8, nt], BF16)
        self.mean = sb(ctx, "mean_sb", [128, nt], F32)
        self.var = sb(ctx, "var_sb", [128, nt], F32)
        self.rstd = sb(ctx, "rstd_sb", [128, nt], F32)
        self.tmp = sb(ctx, "tmp_sb", [128, nt], F32)
        if Y is not None:
            self.Y = Y
            self.ST = ST
        else:
            self.Y = [ps(ctx, "Yps%d" % i, [128, nt]) for i in range(2)]
            self.ST = [ps(ctx, "STps%d" % i, [128, nt]) for i in range(2)]
        self.s_x = [ctx.sem("xld0"), ctx.sem("xld1")]
        self.s_o = [ctx.sem("xst0"), ctx.sem("xst1")]
        self.ev_xfree = [None, None]
        self.ev_x = [None, None]
        self.ev_ybank = [None, None]
        self.ev_stats_rd = None
        self.ev_stats_pe = None
        self.ev_zact = None
        self.ev_stpe_done = None


def xT_view(G, t, nt):
    return G["xT"].rearrange("(m p) t -> p m t", p=128)[:, :, t * nt:(t + 1) * nt]


def epi_load(ctx, G, E, t):
    sl = t % 2
    E.ev_x[sl] = ctx.dma("sp", E.xbuf[sl][:], xT_view(G, t, E.nt), E.s_x[sl], wait=[E.ev_xfree[sl]])


def epi_h(ctx, G, E, t, L, s, hT, wait=()):
    sl = t % 2
    b = t // ((TOK // E.nt) // NB)
    ev = None
    for m in range(8):
        ev = ctx.op("dve", lambda e, m=m: e.tensor_scalar(
            hT[:, m, :], E.xbuf[sl][:, m, :], G["A"][:, L, s, b, m:m + 1],
            G["modT"][:, L, (3 * s) * 8 + m, b:b + 1], op0=ALU.mult, op1=ALU.add),
            wait=[E.ev_x[sl]] + list(wait) if m == 0 else (), sig=(m == 7))
    return ev


def epi_z(ctx, G, E, t, L, s, m, ev_y):
    sl = t % 2
    b = t // ((TOK // E.nt) // NB)
    evz = ctx.op("dve", lambda e: e.scalar_tensor_tensor(
        out=E.xbuf[sl][:, m, :], in0=E.Y[m % 2][:], scalar=G["C"][:, L, s, b, m:m + 1], in1=E.xbuf[sl][:, m, :],
        op0=ALU.mult, op1=ALU.add), wait=[ev_y, E.ev_x[sl]], sig=True)
    E.ev_ybank[m % 2] = evz
    ctx.op("act", lambda e: e.activation(out=E.zb[:, m, :], in_=E.xbuf[sl][:, m, :], func=AF.Copy),
           wait=[evz, E.ev_stpe_done])
    E.ev_zact = ctx.op("act", lambda e: e.activation(out=E.zsq[:, m, :], in_=E.xbuf[sl][:, m, :], func=AF.Square),
                       sig=(m == 7))


def epi_stats_pe(ctx, G, E, t):
    for i, src in enumerate((E.zb, E.zsq)):
        for m in range(8):
            ev = ctx.op("pe", lambda e, i=i, m=m, src=src: e.matmul(
                E.ST[i][:], lhsT=G["ones_bf"][:], rhs=src[:, m, :], start=(m == 0), stop=(m == 7)),
                wait=[E.ev_zact, E.ev_stats_rd] if (i == 0 and m == 0) else (), sig=(i == 1 and m == 7))
    E.ev_stpe_done = ev
    return ev


def epi_norm(ctx, G, E, t, L, s, ev_st):
    sl = t % 2
    ctx.op("dve", lambda e: e.tensor_copy(E.mean[:], E.ST[0][:]), wait=[ev_st])
    ctx.op("dve", lambda e: e.tensor_tensor(E.tmp[:], E.ST[0][:], E.mean[:], op=ALU.mult))
    E.ev_stats_rd = ctx.op("dve", lambda e: e.tensor_tensor(E.var[:], E.ST[1][:], E.tmp[:], op=ALU.subtract), sig=True)
    ev_sd = ctx.op("act", lambda e: e.activation(out=E.rstd[:], in_=E.var[:], func=AF.Sqrt, bias=G["eps"][:, 0:1], scale=1.0),
                   wait=[E.ev_stats_rd], sig=True)
    ctx.op("dve", lambda e: e.reciprocal(E.rstd[:], E.rstd[:]), wait=[ev_sd])
    evn = None
    for m in range(8):
        ctx.op("dve", lambda e, m=m: e.tensor_tensor(E.tmp[:], E.xbuf[sl][:, m, :], E.mean[:], op=ALU.subtract))
        evn = ctx.op("dve", lambda e, m=m: e.tensor_tensor(E.xbuf[sl][:, m, :], E.tmp[:], E.rstd[:], op=ALU.mult),
                     sig=(m == 7))
    eva = None
    for m in range(8):
        eva = ctx.op("act", lambda e, m=m: e.activation(
            out=E.xbuf[sl][:, m, :], in_=E.xbuf[sl][:, m, :], func=AF.Identity,
            scale=G["lng"][:, L, s, m:m + 1], bias=G["lnb"][:, L, s, m:m + 1]),
            wait=[evn] if m == 0 else (), sig=(m == 7))
    E.ev_xfree[sl] = ctx.dma("sp", xT_view(G, t, E.nt), E.xbuf[sl][:], E.s_o[sl], wait=[eva])
    return E.ev_xfree[sl]


def ffn_phase(ctx, G, L, s, w_in, w_out):
    nc = ctx.nc
    import contextlib
    outer = ctx.stack
    with contextlib.ExitStack() as st:
        ctx.stack = st
        Win = sb(ctx, "Win", [128, 8, 2 * DFF], BF16)
        Wout = sb(ctx, "Wout", [128, NJ, D], BF16)
        hT = sb(ctx, "hT", [128, 8, NT], BF16)
        gT = sb(ctx, "gT", [128, NJ, NT], BF16)
        sa = [sb(ctx, "sa%d" % i, [128, NT], F32) for i in range(2)]
        A = [ps(ctx, "Aps%d" % i, [128, NT]) for i in range(2)]
        U = [ps(ctx, "Ups%d" % i, [128, NT]) for i in range(2)]
        E = Epi(ctx, G)
        s_w = ctx.sem("ffnw")
        wv = w_in.rearrange("(k p) n -> p k n", p=128)
        for k in range(8):
            ev_w = ctx.dma("pool", Win[:, k, :], wv[:, k, :], s_w)
        wo = w_out.rearrange("(j p) n -> p j n", p=128)
        for j0 in range(0, NJ, 11):
            ev_w = ctx.dma("pool", Wout[:, j0:j0 + 11, :], wo[:, j0:j0 + 11, :], s_w)

        ev_au = [None, None]
        ev_silu = [None, None]
        ev_g = [None, None]
        ev_hdone = [None]
        ev_glast = [None]
        ev_aulast = [None]
        ev_ylast = [None]

        def au(t):
            for j in range(NJ):
                bk = j % 2
                for k in range(8):
                    ctx.op("pe", lambda e, j=j, k=k, bk=bk: e.matmul(
                        A[bk][:], lhsT=Win[:, k, j * 128:(j + 1) * 128], rhs=hT[:, k, :], start=(k == 0), stop=(k == 7)),
                        wait=[ev_w, ev_hdone[0], ev_silu[bk]] if k == 0 else ())
                for k in range(8):
                    evp = ctx.op("pe", lambda e, j=j, k=k, bk=bk: e.matmul(
                        U[bk][:], lhsT=Win[:, k, DFF + j * 128:DFF + (j + 1) * 128], rhs=hT[:, k, :],
                        start=(k == 0), stop=(k == 7)),
                        wait=[ev_g[bk]] if k == 0 else (), sig=(k == 7))
                ev_silu[bk] = ctx.op("act", lambda e, bk=bk: e.activation(out=sa[bk][:], in_=A[bk][:], func=AF.Silu),
                                     wait=[evp, ev_g[bk]], sig=True)
                ev_g[bk] = ctx.op("dve", lambda e, j=j, bk=bk: e.tensor_tensor(
                    gT[:, j, :], sa[bk][:], U[bk][:], op=ALU.mult), wait=[ev_silu[bk], ev_ylast[0]], sig=True)
            ev_aulast[0] = evp
            ev_glast[0] = ev_g[(NJ - 1) % 2]

        def y(t):
            for m in range(8):
                for j in range(NJ):
                    evy = ctx.op("pe", lambda e, m=m, j=j: e.matmul(
                        E.Y[m % 2][:], lhsT=Wout[:, j, m * 128:(m + 1) * 128], rhs=gT[:, j, :],
                        start=(j == 0), stop=(j == NJ - 1)),
                        wait=[ev_glast[0], E.ev_ybank[m % 2]] if j == 0 else (), sig=(j == NJ - 1))
                epi_z(ctx, G, E, t, L, s, m, evy)
            ev_ylast[0] = evy

        epi_load(ctx, G, E, 0)
        epi_load(ctx, G, E, 1)
        ev_hdone[0] = epi_h(ctx, G, E, 0, L, s, hT)
        last = None
        for t in range(NG):
            au(t)
            if t > 0:
                ev_st = epi_stats_pe(ctx, G, E, t - 1)
                last = epi_norm(ctx, G, E, t - 1, L, s, ev_st)
                if t + 1 < NG:
                    epi_load(ctx, G, E, t + 1)
            y(t)
            if t + 1 < NG:
                ev_hdone[0] = epi_h(ctx, G, E, t + 1, L, s, hT, wait=[ev_aulast[0]])
        ev_st = epi_stats_pe(ctx, G, E, NG - 1)
        last2 = epi_norm(ctx, G, E, NG - 1, L, s, ev_st)
        ctx.barrier([last, last2, ev_st])
    ctx.stack = outer


RH = 4
GAM = [1.0 - 2.0 ** (-5.0 - h) for h in range(RH)]
NT_R = 128
NG_R = int(os.environ.get('NGRUN', TOK // NT_R))
NCH_R = NT_R // 128


def ret_phase(ctx, G, L, w_in, w_out):
    import contextlib
    s = 1
    outer = ctx.stack
    with contextlib.ExitStack() as st:
        ctx.stack = st
        Wr = sb(ctx, "Wr", [128, 8, 6144], BF16)
        Wo = sb(ctx, "Wo", [128, 16, D], BF16)
        hT = sb(ctx, "hTr", [128, 8, NT_R], BF16)
        og = sb(ctx, "og", [128, 2048], BF16)
        qT = sb(ctx, "qT", [128, 8, NT_R], BF16)
        q2T = sb(ctx, "q2T", [128, 8, NT_R], BF16)
        kT = sb(ctx, "kT", [128, 8, NT_R], BF16)
        k2 = sb(ctx, "k2", [128, 1, 1024], BF16)
        vt = sb(ctx, "vt", [128, 1, 2048], BF16)
        sg = sb(ctx, "sg", [128, 1, 2048], BF16)
        on = sb(ctx, "on", [128, 512], F32)
        ogT = sb(ctx, "ogT", [128, 16, NT_R], BF16)
        S = sb(ctx, "S", [128, 8, 512], F32)
        Sb = sb(ctx, "Sb", [128, 8, 512], BF16)
        sTb = sb(ctx, "sTb", [128, 128], BF16)
        st = sb(ctx, "gnst", [128, 8], F32)
        P = [ps(ctx, "Pps%d" % i, [128, 512]) for i in range(2)]
        O = [ps(ctx, "Ops%d" % i, [128, 512]) for i in range(2)]
        SP = ps(ctx, "sTps", [128, 128])
        TP = ps(ctx, "TPps", [128, 4, 128], BF16)
        E = Epi(ctx, G, zb=qT, zsq=q2T, nt=NT_R, Y=[P[0][:, 0:NT_R], P[1][:, 0:NT_R]], ST=[O[0][:, 0:NT_R], O[1][:, 0:NT_R]])
        s_w = ctx.sem("retw")
        wv = w_in.rearrange("(k p) n -> p k n", p=128)
        for k in range(8):
            ev_w = ctx.dma("pool", Wr[:, k, :], wv[:, k, :], s_w)
        wo = w_out.rearrange("(j p) n -> p j n", p=128)
        for j0 in range(0, 16, 8):
            ev_w = ctx.dma("pool", Wo[:, j0:j0 + 8, :], wo[:, j0:j0 + 8, :], s_w)

        evP = [None, None]
        pidx = [0]
        last_store = None
        epi_load(ctx, G, E, 0)
        if NG_R > 1:
            epi_load(ctx, G, E, 1)
        ev_prev_y = None
        ev_sb = None
        ev_Ofree = [None, None]
        for t in range(NG_R):
            sl = t % 2
            first_in_seq = (t % ((TOK // NT_R) // NB) == 0)
            ev_h = epi_h(ctx, G, E, t, L, s, hT, wait=[ev_prev_y])
            ev_s0 = None

            def proj_fm(dst_list, col0):
                evs = None
                for oc in range(8):
                    bk = pidx[0] % 2
                    pidx[0] += 1
                    for k in range(8):
                        evp = ctx.op("pe", lambda e, oc=oc, k=k, bk=bk: e.matmul(
                            P[bk][:, 0:NT_R], lhsT=Wr[:, k, col0 + oc * 128:col0 + (oc + 1) * 128], rhs=hT[:, k, :],
                            start=(k == 0), stop=(k == 7)), wait=[ev_w, ev_h, evP[bk]] if k == 0 else (), sig=(k == 7))
                    for (eng, fn) in dst_list:
                        evP[bk] = ctx.op(eng, lambda e, fn=fn, oc=oc, bk=bk: fn(e, oc, P[bk][:, 0:NT_R]), wait=[evp], sig=True)
                        evs = evP[bk]
                    if len(dst_list) == 2:
                        evP[bk] = evs
                return evs

            def q_act(e, oc, src):
                return e.activation(out=qT[:, oc, :], in_=src, func=AF.Copy)

            def q_dve(e, oc, src):
                return e.tensor_tensor(q2T[:, oc, :], src, G["dqT"][:, oc // 2, :], op=ALU.mult)

            def k_act(e, oc, src):
                return e.activation(out=kT[:, oc, :], in_=src, func=AF.Copy)

            ev_qa = None
            for oc in range(8):
                bk = pidx[0] % 2
                pidx[0] += 1
                for k in range(8):
                    evp = ctx.op("pe", lambda e, oc=oc, k=k, bk=bk: e.matmul(
                        P[bk][:, 0:NT_R], lhsT=Wr[:, k, oc * 128:(oc + 1) * 128], rhs=hT[:, k, :],
                        start=(k == 0), stop=(k == 7)), wait=[ev_w, ev_h, evP[bk]] if k == 0 else (), sig=(k == 7))
                ev_a = ctx.op("act", lambda e, oc=oc, bk=bk: q_act(e, oc, P[bk][:, 0:NT_R]), wait=[evp], sig=True)
                for c in range(NCH_R):
                    evP[bk] = ctx.op("dve", lambda e, oc=oc, bk=bk, c=c: e.tensor_tensor(
                        q2T[:, oc, c * 128:(c + 1) * 128], P[bk][:, c * 128:(c + 1) * 128], G["dqT"][:, oc // 2, :], op=ALU.mult),
                        wait=[evp, ev_a], sig=(c == NCH_R - 1))
            ev_q = evP[(pidx[0] - 1) % 2]
            ev_k = proj_fm([("act", k_act)], 1024)

            if ev_s0 is not None:
                ev_sb = ev_s0
            ev_og_free = None
            for c in range(NCH_R):
                cs = slice(c * 128, (c + 1) * 128)
                for nchunk in range(2 + 4 + 4):
                    bk = pidx[0] % 2
                    pidx[0] += 1
                    if nchunk < 2:
                        col0 = 1024 + nchunk * 512
                    elif nchunk < 6:
                        col0 = 2048 + (nchunk - 2) * 512
                    else:
                        col0 = 4096 + (nchunk - 6) * 512
                    for k in range(8):
                        evp = ctx.op("pe", lambda e, c=c, k=k, bk=bk, col0=col0: e.matmul(
                            P[bk][:], lhsT=hT[:, k, c * 128:(c + 1) * 128], rhs=Wr[:, k, col0:col0 + 512],
                            start=(k == 0), stop=(k == 7)), wait=[evP[bk]] if k == 0 else (), sig=(k == 7))
                    if nchunk < 2:
                        for hh in range(2):
                            h = nchunk * 2 + hh
                            evP[bk] = ctx.op("dve", lambda e, c=c, h=h, hh=hh, bk=bk: e.tensor_scalar(
                                k2[:, 0, h * 256:(h + 1) * 256], P[bk][:, hh * 256:(hh + 1) * 256],
                                G["dk"][:, h:h + 1], None, op0=ALU.mult), wait=[evp], sig=(hh == 1))
                        ev_k2 = evP[bk]
                    elif nchunk < 6:
                        evP[bk] = ctx.op("act", lambda e, c=c, n=nchunk - 2, bk=bk: e.activation(
                            out=vt[:, 0, n * 512:(n + 1) * 512], in_=P[bk][:], func=AF.Copy), wait=[evp], sig=True)
                    else:
                        evP[bk] = ctx.op("act", lambda e, c=c, n=nchunk - 6, bk=bk: e.activation(
                            out=sg[:, 0, n * 512:(n + 1) * 512], in_=P[bk][:], func=AF.Silu), wait=[evp], sig=True)
                    ev_tok = evP[bk]


                for hp in range(2):
                    ev_o = [None, None]
                    for hh in range(2):
                        h = hp * 2 + hh
                        for dd in range(2):
                            evs = ctx.op("pe", lambda e, h=h, dd=dd, cs=cs: e.matmul(
                                SP[:], lhsT=kT[:, 2 * h + dd, cs], rhs=qT[:, 2 * h + dd, cs], start=(dd == 0), stop=(dd == 1)),
                                wait=[ev_q, ev_k] if dd == 0 else (), sig=(dd == 1))
                        ev_st = ctx.op("dve", lambda e, h=h: e.tensor_tensor(sTb[:], SP[:], G["DT"][:, h, :], op=ALU.mult),
                                       wait=[evs], sig=True)
                        fresh = first_in_seq and c == 0
                        ev_o[hh] = ctx.op("pe", lambda e, h=h, hh=hh, c=c, fresh=fresh: e.matmul(
                            O[hh][:], lhsT=sTb[:], rhs=vt[:, 0, h * 512:(h + 1) * 512], start=True, stop=fresh),
                            wait=[ev_st, ev_sb, ev_og_free, ev_Ofree[hh], ev_tok, ev_k2, E.ev_stats_rd], sig=fresh)
                        for dd in range(2):
                            if fresh:
                                break
                            ev_o[hh] = ctx.op("pe", lambda e, h=h, hh=hh, dd=dd, cs=cs: e.matmul(
                                O[hh][:], lhsT=q2T[:, 2 * h + dd, cs], rhs=Sb[:, 2 * h + dd, :], start=False, stop=(dd == 1)),
                                sig=(dd == 1))
                        for dd in range(2):
                            bk = pidx[0] % 2
                            pidx[0] += 1
                            evp = ctx.op("pe", lambda e, h=h, dd=dd, c=c, bk=bk: e.matmul(
                                P[bk][:], lhsT=k2[:, 0, h * 256 + dd * 128:h * 256 + (dd + 1) * 128],
                                rhs=vt[:, 0, h * 512:(h + 1) * 512], start=True, stop=True), wait=[evP[bk]], sig=True)
                            if fresh:
                                evP[bk] = ctx.op("dve", lambda e, h=h, dd=dd, bk=bk: e.tensor_copy(
                                    S[:, 2 * h + dd, :], P[bk][:]), wait=[evp], sig=True)
                            else:
                                evP[bk] = ctx.op("dve", lambda e, h=h, dd=dd, bk=bk: e.scalar_tensor_tensor(
                                    out=S[:, 2 * h + dd, :], in0=S[:, 2 * h + dd, :], scalar=GAM[h] ** 128, in1=P[bk][:],
                                    op0=ALU.mult, op1=ALU.add), wait=[evp], sig=True)
                            ev_sb = ctx.op("act", lambda e, h=h, dd=dd: e.activation(
                                out=Sb[:, 2 * h + dd, :], in_=S[:, 2 * h + dd, :], func=AF.Copy), wait=[evP[bk], evp], sig=True)
                        ev_of = ctx.op("dve", lambda e, hh=hh: e.tensor_scalar(
                            on[:], O[hh][:], 1.0, 0.0, op0=ALU.mult, op1=ALU.add, accum_out=st[:, 0:1]),
                            wait=[ev_o[hh]], sig=True)
                        ev_acc = ctx.op("dve", lambda e, h=h: e.scalar_tensor_tensor(
                            out=og[:, h * 512:(h + 1) * 512], in0=on[:], scalar=1.0, in1=on[:],
                            op0=ALU.mult, op1=ALU.mult, accum_out=st[:, 1:2]), sig=True)
                        ctx.op("dve", lambda e: e.tensor_scalar(st[:, 2:4], st[:, 0:2], 1.0 / 512.0, None, op0=ALU.mult),
                               wait=[ev_acc])
                        ctx.op("dve", lambda e: e.tensor_tensor(st[:, 4:5], st[:, 2:3], st[:, 2:3], op=ALU.mult))
                        ev_var = ctx.op("dve", lambda e: e.tensor_tensor(st[:, 5:6], st[:, 3:4], st[:, 4:5], op=ALU.subtract), sig=True)
                        ev_sd = ctx.op("act", lambda e: e.activation(out=st[:, 6:7], in_=st[:, 5:6], func=AF.Sqrt,
                                                                      bias=G["gneps"][:, 0:1], scale=1.0), wait=[ev_var], sig=True)
                        ctx.op("dve", lambda e: e.reciprocal(st[:, 6:7], st[:, 6:7]), wait=[ev_sd])
                        ctx.op("dve", lambda e: e.tensor_scalar(
                            on[:], on[:], st[:, 2:3], st[:, 6:7], op0=ALU.subtract, op1=ALU.mult))
                        ev_og = ctx.op("dve", lambda e, h=h, c=c: e.tensor_tensor(
                            og[:, h * 512:(h + 1) * 512], on[:], sg[:, 0, h * 512:(h + 1) * 512], op=ALU.mult), sig=True)
                        ev_Ofree[hh] = ev_of
                ev_tr = None
                for f4 in range(4):
                    for ff in range(4):
                        f = f4 * 4 + ff
                        evt = ctx.op("pe", lambda e, f=f, ff=ff: e.transpose(TP[:, ff, :], og[:, f * 128:(f + 1) * 128], G["ident"][:]),
                                     wait=[ev_og, ev_tr] if ff == 0 else (), sig=(ff == 3))
                    ev_tr = ctx.op("act", lambda e, f4=f4, cs=cs: e.activation(
                        out=ogT[:, f4 * 4:(f4 + 1) * 4, cs], in_=TP[:], func=AF.Copy), wait=[evt], sig=True)
                ev_og_free = ev_tr
            for m in range(8):
                for f in range(16):
                    evy = ctx.op("pe", lambda e, m=m, f=f: e.matmul(
                        E.Y[m % 2][:], lhsT=Wo[:, f, m * 128:(m + 1) * 128], rhs=ogT[:, f, :],
                        start=(f == 0), stop=(f == 15)),
                        wait=[ev_tr, E.ev_ybank[m % 2], evP[m % 2]] if f == 0 else (), sig=(f == 15))
                epi_z(ctx, G, E, t, L, s, m, evy)
                evP[m % 2] = E.ev_ybank[m % 2]
            ev_prev_y = evy
            ctx.op("pe", None, wait=[ev_og_free, ev_og])
            ev_st = epi_stats_pe(ctx, G, E, t)
            last_store = epi_norm(ctx, G, E, t, L, s, ev_st)
            if t + 2 < NG_R:
                epi_load(ctx, G, E, t + 2)
        evd = None
        if "dbg" in G:
            evd = dbg_dump(ctx, G, [("hT", hT[:, 0, :], 128), ("kT0", kT[:, 0, :], 128), ("k2", k2[:, 0, :], 1024),
                                    ("vt", vt[:, 0, :], 2048), ("sg", sg[:, 0, :], 2048), ("og", og[:, :], 2048),
                                    ("on", on[:, :], 512), ("ogT0", ogT[:, 0, :], 128), ("S0", S[:, 0, :], 512),
                                    ("Sb0", Sb[:, 0, :], 512), ("sTb", sTb[:, :], 128), ("st", st[:, :], 8),
                                    ("mean", E.mean[:, :], 128), ("var", E.var[:, :], 128), ("rstd", E.rstd[:, :], 128),
                                    ("xb0", E.xbuf[0][:, 0, :], 128), ("xb7", E.xbuf[0][:, 7, :], 128),
                                    ("zb0", E.zb[:, 0, :], 128), ("zsq0", E.zsq[:, 0, :], 128), ("zb7", E.zb[:, 7, :], 128),
                                    ("ogT15", ogT[:, 15, :], 128)],
                           wait=[last_store, ev_st])
        ctx.barrier([last_store, E.ev_xfree[0], E.ev_xfree[1], ev_st, evd])
    ctx.stack = outer


NHEAD = 16
HPG = 4
DH = 64
NCMP = 127
KA = 97
SLOPES = [2.0 ** (-8.0 * (i + 1) / 16.0) for i in range(NHEAD)]
NQB = SEQ // 128
CLAMP = 40.0
NEGM = 30000.0


def nsa_phase(ctx, G, L, W):
    import contextlib
    s = 1
    nt = 128
    outer = ctx.stack
    ctx.strict = True
    with contextlib.ExitStack() as st:
        ctx.stack = st
        Wn = sb(ctx, "Wn", [128, 8, 2608], BF16)
        Wno = sb(ctx, "Wno", [128, 8, D], BF16)
        hT = sb(ctx, "hTn", [128, 8, SEQ], BF16)
        oT = sb(ctx, "oTn", [128, 8, SEQ], BF16)
        W2k = sb(ctx, "W2k", [128, 2, DH], BF16)
        W2v = sb(ctx, "W2v", [128, 2, DH], BF16)
        peT = sb(ctx, "peT", [64, 2, 32], BF16)
        kcA = sb(ctx, "kcA", [KA, 4, 128], BF16)
        vcA = sb(ctx, "vcA", [128, 4, 97], BF16)
        Bk = sb(ctx, "Bk", [128, NHEAD, NQB], F32)
        Bc = sb(ctx, "Bc", [128, NHEAD], F32)
        cmask = sb(ctx, "cmask", [128, NQB, 128], BF16)
        tri = sb(ctx, "tri", [128, 128], BF16)
        triU = sb(ctx, "triU", [128, 128], BF16)
        topk = sb(ctx, "topk", [128, NQB, 2, 32], F32)
        P = [ps(ctx, "nP%d" % i, [128, 512]) for i in range(2)]
        SPS = [ps(ctx, "nS%d" % i, [128, 128]) for i in range(2)]
        OC = ps(ctx, "nOC", [128, 4, 97])
        OS = ps(ctx, "nOS", [128, 4, 65])
        OW = ps(ctx, "nOW", [128, 4, 65])
        TP = ps(ctx, "nTP", [128, 2, 128], BF16)
        E = Epi(ctx, G, nt=nt, Y=[P[0][:, 0:nt], P[1][:, 0:nt]], ST=[SPS[0][:, 0:nt], SPS[1][:, 0:nt]])

        s_w = ctx.sem("nsaw")
        s_t = ctx.sem("nsat")
        wv = W["w_in"].rearrange("(k p) n -> p k n", p=128)
        for k in range(8):
            ctx.dma("pool", Wn[:, k, :], wv[:, k, :], s_w)
        ctx.dma("pool", Wno[:], W["w_out"].rearrange("(k p) n -> p k n", p=128), s_w)
        ctx.dma("pool", W2k[:], W["ck_w2"].rearrange("(c p) d -> p c d", p=128), s_w)
        ctx.dma("pool", W2v[:], W["cv_w2"].rearrange("(c p) d -> p c d", p=128), s_w)
        ctx.dma("pool", peT[:, 0, :], W["pekT"], s_w)
        ctx.dma("pool", peT[:, 1, :], W["pevT"], s_w)
        ctx.dma("pool", cmask[:], G["c_cmask"], s_w)
        ctx.dma("pool", tri[:], G["c_tri"], s_w)
        ctx.dma("pool", triU[:], G["c_triU"], s_w)
        for g in range(4):
            ctx.dma("pool", vcA[:, g, 64:96], G["c_ovl"], s_w)
        ev_w = ctx.dma("pool", kcA[96:97, :, :], G["c_ones4"], s_w)
        ctx.dma("sp", Bk[:], G["c_Bk"], s_t)
        ctx.dma("sp", Bc[:], G["c_Bc"], s_t)
        ev_t = ctx.dma("sp", topk[:], G["c_topk"], s_t)
        ctx.op("dve", lambda e: e.memset(kcA[64:96, :, :], 0.0))
        ev_ms = ctx.op("dve", lambda e: e.memset(vcA[:, :, 96:97], 1.0), sig=True)
        ctx.barrier([ev_w, ev_t, ev_ms])

        pidx = [0]
        evP = [None, None]
        sidx = [0]
        evS = [None, None]

        def pbank():
            bk = pidx[0] % 2
            pidx[0] += 1
            return bk

        for b in range(int(os.environ.get('NSA_DBG_B', NB))):
            ev_h = None
            for t in range(NQB):
                tg = b * NQB + t
                sl = tg % 2
                epi_load(ctx, G, E, tg)
                for m in range(8):
                    ev_h = ctx.op("dve", lambda e, m=m, sl=sl, t=t: e.tensor_scalar(
                        hT[:, m, t * 128:(t + 1) * 128], E.xbuf[sl][:, m, :], G["A"][:, L, s, b, m:m + 1],
                        G["modT"][:, L, (3 * s) * 8 + m, b:b + 1], op0=ALU.mult, op1=ALU.add),
                        wait=[E.ev_x[sl]] if m == 0 else (), sig=(m == 7))
                E.ev_xfree[sl] = ev_h

            with contextlib.ExitStack() as st1:
                ctx.stack = st1
                W1 = [sb(ctx, "W1k", [64, 32, 256], BF16), sb(ctx, "W1v", [64, 32, 256], BF16)]
                X = [sb(ctx, "kcX", [64, SEQ], BF16), sb(ctx, "vcX", [64, SEQ], BF16)]
                hid = sb(ctx, "hid", [128, 2, 2, 128], BF16)
                bkv = sb(ctx, "bkv", [128, 2, 2], F32)
                u = sb(ctx, "gu", [128, 128], F32)
                w_ = sb(ctx, "gw", [128, 128], F32)
                s_w1 = ctx.sem("nsaw1")
                ctx.dma("pool", W1[0][:], W["ck_w1"].rearrange("(l d) c -> d l c", d=64), s_w1)
                ev_w1 = ctx.dma("pool", W1[1][:], W["cv_w1"].rearrange("(l d) c -> d l c", d=64), s_w1)
                ev_b = None
                for kv in range(2):
                    for cc in range(2):
                        bk = pbank()
                        for l in range(32):
                            evp = ctx.op("pe", lambda e, kv=kv, cc=cc, l=l, bk=bk: e.matmul(
                                P[bk][:, 0:1], lhsT=W1[kv][:, l, cc * 128:(cc + 1) * 128], rhs=peT[:, kv, l:l + 1],
                                start=(l == 0), stop=(l == 31)), wait=[ev_w1, evP[bk]] if l == 0 else (), sig=(l == 31))
                        evP[bk] = ctx.op("dve", lambda e, kv=kv, cc=cc, bk=bk: e.tensor_copy(bkv[:, kv, cc:cc + 1], P[bk][:, 0:1]),
                                         wait=[evp], sig=True)
                        ev_b = evP[bk]
                for g in range(4):
                    for kv in range(2):
                        col = D + kv * 256 + g * 64
                        for sl4 in range(4):
                            bk = pbank()
                            for k in range(8):
                                evp = ctx.op("pe", lambda e, k=k, bk=bk, col=col, sl4=sl4: e.matmul(
                                    P[bk][0:64, :], lhsT=Wn[:, k, col:col + 64], rhs=hT[:, k, sl4 * 512:(sl4 + 1) * 512],
                                    start=(k == 0), stop=(k == 7)), wait=[ev_h, evP[bk]] if k == 0 else (), sig=(k == 7))
                            evP[bk] = ctx.op("act", lambda e, kv=kv, bk=bk, sl4=sl4: e.activation(
                                out=X[kv][:, sl4 * 512:(sl4 + 1) * 512], in_=P[bk][0:64, :], func=AF.Copy), wait=[evp], sig=True)
                    ev_x = evP[(pidx[0] - 1) % 2]
                    for kv in range(2):
                        for cc in range(2):
                            bk = pbank()
                            for l in range(32):
                                evp = ctx.op("pe", lambda e, kv=kv, cc=cc, l=l, bk=bk: e.matmul(
                                    P[bk][:, 0:NCMP], lhsT=W1[kv][:, l, cc * 128:(cc + 1) * 128],
                                    rhs=X[kv][:, l:l + 16 * (NCMP - 1) + 1:16], start=(l == 0), stop=(l == 31)),
                                    wait=[ev_x, evP[bk]] if l == 0 else (), sig=(l == 31))
                            evP[bk] = ctx.op("dve", lambda e, kv=kv, cc=cc, bk=bk: e.tensor_scalar(
                                u[:, 0:NCMP], P[bk][:, 0:NCMP], bkv[:, kv, cc:cc + 1], None, op0=ALU.add), wait=[evp, ev_b], sig=True)
                            ctx.op("dve", lambda e: e.tensor_tensor(w_[:, 0:NCMP], u[:, 0:NCMP], u[:, 0:NCMP], op=ALU.mult))
                            ctx.op("dve", lambda e: e.tensor_scalar(w_[:, 0:NCMP], w_[:, 0:NCMP], 0.044715, 1.0, op0=ALU.mult, op1=ALU.add))
                            ev1 = ctx.op("dve", lambda e: e.tensor_tensor(w_[:, 0:NCMP], w_[:, 0:NCMP], u[:, 0:NCMP], op=ALU.mult), sig=True)
                            ev2 = ctx.op("act", lambda e: e.activation(out=w_[:, 0:NCMP], in_=w_[:, 0:NCMP], func=AF.Tanh,
                                                                     scale=0.7978845608028654), wait=[ev1], sig=True)
                            ctx.op("dve", lambda e: e.tensor_scalar(w_[:, 0:NCMP], w_[:, 0:NCMP], 1.0, 0.5, op0=ALU.add, op1=ALU.mult),
                                   wait=[ev2])
                            ev_hid = ctx.op("dve", lambda e, kv=kv, cc=cc: e.tensor_tensor(
                                hid[:, kv, cc, 0:NCMP], w_[:, 0:NCMP], u[:, 0:NCMP], op=ALU.mult), sig=True)
                    bk = pbank()
                    for cc in range(2):
                        evp = ctx.op("pe", lambda e, cc=cc, bk=bk: e.matmul(
                            P[bk][0:64, 0:NCMP], lhsT=W2k[:, cc, :], rhs=hid[:, 0, cc, 0:NCMP], start=(cc == 0), stop=(cc == 1)),
                            wait=[ev_hid, evP[bk]] if cc == 0 else (), sig=(cc == 1))
                    evP[bk] = ctx.op("act", lambda e, g=g, bk=bk: e.activation(
                        out=kcA[0:64, g, 0:NCMP], in_=P[bk][0:64, 0:NCMP], func=AF.Copy), wait=[evp], sig=True)
                    bk = pbank()
                    for cc in range(2):
                        evp = ctx.op("pe", lambda e, cc=cc, bk=bk: e.matmul(
                            P[bk][0:NCMP, 0:64], lhsT=hid[:, 1, cc, 0:NCMP], rhs=W2v[:, cc, :], start=(cc == 0), stop=(cc == 1)),
                            wait=[ev_hid, evP[bk]] if cc == 0 else (), sig=(cc == 1))
                    evP[bk] = ctx.op("act", lambda e, g=g, bk=bk: e.activation(
                        out=vcA[0:NCMP, g, 0:64], in_=P[bk][0:NCMP, 0:64], func=AF.Copy), wait=[evp], sig=True)
                ctx.barrier([evP[0], evP[1]])
            ctx.stack = st

            with contextlib.ExitStack() as st2:
                ctx.stack = st2
                qA = [sb(ctx, "qA%d" % i, [KA, SEQ], BF16) for i in range(4)]
                ksA = sb(ctx, "ksA", [KA, SEQ], BF16)
                kwA = sb(ctx, "kwA", [KA, SEQ], BF16)
                vsA = sb(ctx, "vsA", [128, NQB, 65], BF16)
                vwA = sb(ctx, "vwA", [128, NQB, 65], BF16)
                gat = sb(ctx, "gat", [128, NQB, 12], F32)
                pT = [sb(ctx, "pT%d" % i, [128, 128], BF16) for i in range(2)]
                tmpf = sb(ctx, "tmpf", [128, 128], F32)
                rz = sb(ctx, "rz", [128, 3, 4], F32)
                imp = sb(ctx, "imp", [128, 32], F32)
                wk = sb(ctx, "wk", [128, 32], F32)
                m8 = sb(ctx, "m8", [128, 16], F32)
                nm = sb(ctx, "nm", [128, 96], BF16)
                oacc = sb(ctx, "oacc", [128, 256], F32)
                oab = sb(ctx, "oab", [128, 256], BF16)
                s_c = ctx.sem("nsac")
                ctx.dma("pool", ksA[64:96, :], G["c_E"], s_c)
                ctx.dma("pool", ksA[96:97, :], G["c_ones_row"], s_c)
                ev_c = ctx.dma("pool", kwA[96:97, :], G["c_ones_row"], s_c)
                ctx.op("dve", lambda e: e.memset(kwA[64:96, :], 0.0))
                ctx.op("dve", lambda e: e.memset(nm[:, 0:64], 0.0))
                ctx.op("dve", lambda e: e.memset(vsA[:, :, 64:65], 1.0))
                ev_m1 = ctx.op("dve", lambda e: e.memset(vwA[:, :, 64:65], 1.0), sig=True)
                pti = [0]
                evPT = [None, None]
                ev_oacc_rd = None
                ev_oc_rd = None
                ev_os_rd = None
                ev_ow_rd = None
                ev_tp_rd = None
                for g in range(int(os.environ.get('NSA_DBG_G', 4))):
                    for hp in range(4):
                        ctx.dma("pool", qA[hp][96:97, :], G["c_qshift"][4 * g + hp:4 * g + hp + 1, :], s_c,
                                wait=[ev_ow_rd, ev_os_rd, ev_oc_rd])
                        ev_c = (s_c, s_c.v)
                        ev_m1 = ctx.op("dve", lambda e, hp=hp: e.memset(qA[hp][64:96, :], 0.0), sig=True)
                    for (dst, jj) in ((ksA, 2), (kwA, 4)):
                        col = D + jj * 256 + g * 64
                        for sl4 in range(4):
                            bk = pbank()
                            for k in range(8):
                                evp = ctx.op("pe", lambda e, k=k, bk=bk, col=col, sl4=sl4: e.matmul(
                                    P[bk][0:64, :], lhsT=Wn[:, k, col:col + 64], rhs=hT[:, k, sl4 * 512:(sl4 + 1) * 512],
                                    start=(k == 0), stop=(k == 7)), wait=[evP[bk]] if k == 0 else (), sig=(k == 7))
                            evP[bk] = ctx.op("act", lambda e, dst=dst, bk=bk, sl4=sl4: e.activation(
                                out=dst[0:64, sl4 * 512:(sl4 + 1) * 512], in_=P[bk][0:64, :], func=AF.Copy), wait=[evp], sig=True)
                    for hp in range(4):
                        col = (4 * g + hp) * 64
                        for sl4 in range(4):
                            bk = pbank()
                            for k in range(8):
                                evp = ctx.op("pe", lambda e, k=k, bk=bk, col=col, sl4=sl4: e.matmul(
                                    P[bk][0:64, :], lhsT=Wn[:, k, col:col + 64], rhs=hT[:, k, sl4 * 512:(sl4 + 1) * 512],
                                    start=(k == 0), stop=(k == 7)), wait=[evP[bk]] if k == 0 else (), sig=(k == 7))
                            evP[bk] = ctx.op("act", lambda e, hp=hp, bk=bk, sl4=sl4: e.activation(
                                out=qA[hp][0:64, sl4 * 512:(sl4 + 1) * 512], in_=P[bk][0:64, :], func=AF.Copy, scale=DH ** -0.5),
                                wait=[evp], sig=True)
                    cvs = D + 3 * 256 + g * 64
                    cvw = D + 5 * 256 + g * 64
                    cg = D + 6 * 256 + g * 12
                    for t in range(NQB):
                        bk = pbank()
                        for (c0, o0, n) in ((cvs, 0, 64), (cvw, 64, 64), (cg, 128, 12)):
                            for k in range(8):
                                evp = ctx.op("pe", lambda e, k=k, bk=bk, c0=c0, o0=o0, n=n, t=t: e.matmul(
                                    P[bk][:, o0:o0 + n], lhsT=hT[:, k, t * 128:(t + 1) * 128], rhs=Wn[:, k, c0:c0 + n],
                                    start=(k == 0), stop=(k == 7)), wait=[evP[bk]] if (k == 0 and o0 == 0) else (),
                                    sig=(k == 7 and o0 == 128))
                        ctx.op("act", lambda e, bk=bk, t=t: e.activation(out=vsA[:, t, 0:64], in_=P[bk][:, 0:64], func=AF.Copy),
                               wait=[evp, ev_ow_rd, ev_os_rd, ev_oc_rd])
                        ctx.op("act", lambda e, bk=bk, t=t: e.activation(out=vwA[:, t, 0:64], in_=P[bk][:, 64:128], func=AF.Copy))
                        evP[bk] = ctx.op("act", lambda e, bk=bk, t=t: e.activation(
                            out=gat[:, t, :], in_=P[bk][:, 128:140], func=AF.Sigmoid), sig=True)
                    ev_proj = evP[(pidx[0] - 1) % 2]

                    def score_tile(KT, kcols, hp, qb, bias_ap, mask_ap, clamp):
                        sb_ = sidx[0] % 2
                        sidx[0] += 1
                        nk = kcols[1] - kcols[0] if isinstance(kcols, tuple) else None
                        evs = ctx.op("pe", lambda e: e.matmul(
                            SPS[sb_][0:KT[1], :], lhsT=KT[0], rhs=qA[hp][:, qb * 128:(qb + 1) * 128], start=True, stop=True),
                            wait=[evS[sb_], ev_proj, ev_c, ev_m1], sig=True)
                        pi = pti[0] % 2
                        pti[0] += 1
                        nkk = KT[1]
                        if clamp:
                            evd = ctx.op("dve", lambda e: e.tensor_scalar(
                                tmpf[0:nkk, :], SPS[sb_][0:nkk, :], bias_ap, CLAMP, op0=ALU.add, op1=ALU.min), wait=[evs], sig=True)
                            evS[sb_] = evd
                            eva = ctx.op("act", lambda e: e.activation(out=tmpf[0:nkk, :], in_=tmpf[0:nkk, :], func=AF.Exp),
                                         wait=[evd], sig=True)
                            evp_ = ctx.op("dve", lambda e: e.tensor_tensor(pT[pi][0:nkk, :], tmpf[0:nkk, :], mask_ap, op=ALU.mult),
                                          wait=[eva, evPT[pi]], sig=True)
                        else:
                            eva = ctx.op("act", lambda e: e.activation(out=pT[pi][0:nkk, :], in_=SPS[sb_][0:nkk, :], func=AF.Exp,
                                                                      bias=bias_ap, scale=1.0), wait=[evs, evPT[pi]], sig=True)
                            evS[sb_] = eva
                            evp_ = eva
                            if mask_ap is not None:
                                evp_ = ctx.op("dve", lambda e: e.tensor_tensor(pT[pi][0:nkk, :], pT[pi][0:nkk, :], mask_ap, op=ALU.mult),
                                              wait=[eva], sig=True)
                        return pi, evp_

                    for qb in range(NQB):
                        for hp in range(4):
                            hd = 4 * g + hp
                            pi, evp_ = score_tile((kcA[:, g, 0:NCMP], NCMP), None, hp, qb, Bc[0:NCMP, hd:hd + 1],
                                                  cmask[0:NCMP, qb, :], True)
                            evPT[pi] = ctx.op("pe", lambda e, hp=hp, pi=pi: e.matmul(
                                OC[:, hp, :], lhsT=pT[pi][0:NCMP, :], rhs=vcA[0:NCMP, g, :], start=True, stop=True),
                                wait=[evp_, ev_oc_rd], sig=True)
                        ev_oc = evPT[pi]
                        ctx.op("dve", lambda e: e.tensor_scalar_max(rz[:, 0, :], OC[:, :, 96], 1e-30), wait=[ev_oc, ev_oacc_rd])
                        ctx.op("dve", lambda e: e.reciprocal(rz[:, 0, :], rz[:, 0, :]))
                        ctx.op("dve", lambda e: e.tensor_scalar(imp[:], OC[:, 0, 64:96], rz[:, 0, 0:1], None, op0=ALU.mult))
                        for hp in range(1, 4):
                            ctx.op("dve", lambda e, hp=hp: e.scalar_tensor_tensor(
                                out=imp[:], in0=OC[:, hp, 64:96], scalar=rz[:, 0, hp:hp + 1], in1=imp[:], op0=ALU.mult, op1=ALU.add))
                        ctx.op("dve", lambda e, qb=qb: e.tensor_tensor(rz[:, 0, :], rz[:, 0, :], gat[:, qb, 0:12:3], op=ALU.mult))
                        for hp in range(4):
                            ev_oc_rd = ctx.op("dve", lambda e, hp=hp: e.tensor_scalar(
                                oacc[:, hp * 64:(hp + 1) * 64], OC[:, hp, 0:64], rz[:, 0, hp:hp + 1], None, op0=ALU.mult), sig=(hp == 3))
                        ctx.op("dve", lambda e, qb=qb: e.tensor_tensor(imp[:], imp[:], topk[:, qb, 0, :], op=ALU.mult))
                        ctx.op("dve", lambda e, qb=qb: e.tensor_tensor(imp[:], imp[:], topk[:, qb, 1, :], op=ALU.add))
                        ev_i = ctx.op("dve", lambda e, qb=qb: e.tensor_copy(imp[:], imp[:]), sig=True)
                        ev_a1 = ctx.op("dve", lambda e: e.max(out=m8[:, 0:8], in_=imp[:]), wait=[ev_i], sig=True)
                        ev_a2 = ctx.op("dve", lambda e: e.match_replace(out=wk[:], in_to_replace=m8[:, 0:8], in_values=imp[:], imm_value=-2.0),
                                       wait=[ev_a1], sig=True)
                        ev_a3 = ctx.op("dve", lambda e: e.max(out=m8[:, 8:16], in_=wk[:]), wait=[ev_a2], sig=True)
                        ctx.op("dve", lambda e: e.tensor_scalar(wk[:], imp[:], m8[:, 15:16], None, op0=ALU.is_ge), wait=[ev_a3])
                        ev_nm = ctx.op("dve", lambda e: e.tensor_scalar(nm[:, 64:96], wk[:], -1.0, NEGM, op0=ALU.add, op1=ALU.mult),
                                       wait=[ev_tp_rd], sig=True)
                        for hp in range(4):
                            hd = 4 * g + hp
                            k0 = max(0, qb - 4)
                            for kt in range(k0, qb + 1):
                                diag = (kt == qb)
                                far = (kt == qb - 4)
                                pi, evp_ = score_tile((kwA[:, kt * 128:(kt + 1) * 128], 128), None, hp, qb, Bk[:, hd, kt:kt + 1],
                                                      tri[:] if diag else (triU[:] if far else None), diag)
                                evPT[pi] = ctx.op("pe", lambda e, hp=hp, pi=pi, kt=kt, qb=qb, k0=k0: e.matmul(
                                    OW[:, hp, :], lhsT=pT[pi][:], rhs=vwA[:, kt, :], start=(kt == k0), stop=(kt == qb)),
                                    wait=[evp_, ev_ow_rd], sig=True)
                        ev_ow = evPT[pi]
                        evt = ctx.op("pe", lambda e: e.transpose(TP[0:96, 0, :], nm[:], G["ident"][:]), wait=[ev_nm, ev_tp_rd], sig=True)
                        for hp in range(4):
                            ev_m1 = ctx.op("act", lambda e, hp=hp, qb=qb: e.activation(
                                out=qA[hp][64:96, qb * 128:(qb + 1) * 128], in_=TP[64:96, 0, :], func=AF.Copy), wait=[evt], sig=True)
                        ev_tp_rd = ev_m1
                        for hp in range(4):
                            hd = 4 * g + hp
                            for kt in range(qb + 1):
                                diag = (kt == qb)
                                pi, evp_ = score_tile((ksA[:, kt * 128:(kt + 1) * 128], 128), None, hp, qb, Bk[:, hd, kt:kt + 1],
                                                      tri[:] if diag else None, diag)
                                evPT[pi] = ctx.op("pe", lambda e, hp=hp, pi=pi, kt=kt, qb=qb: e.matmul(
                                    OS[:, hp, :], lhsT=pT[pi][:], rhs=vsA[:, kt, :], start=(kt == 0), stop=(kt == qb)),
                                    wait=[evp_, ev_os_rd], sig=True)
                        ev_os = evPT[pi]
                        for (bi, OX, ev_ox) in ((1, OS, ev_os), (2, OW, ev_ow)):
                            ctx.op("dve", lambda e, bi=bi, OX=OX: e.tensor_scalar_max(rz[:, bi, :], OX[:, :, 64], 1e-30), wait=[ev_ox])
                            ctx.op("dve", lambda e, bi=bi: e.reciprocal(rz[:, bi, :], rz[:, bi, :]))
                            ctx.op("dve", lambda e, bi=bi, qb=qb: e.tensor_tensor(rz[:, bi, :], rz[:, bi, :], gat[:, qb, bi:12:3], op=ALU.mult))
                            for hp in range(4):
                                evr = ctx.op("dve", lambda e, hp=hp, bi=bi, OX=OX: e.scalar_tensor_tensor(
                                    out=oacc[:, hp * 64:(hp + 1) * 64], in0=OX[:, hp, 0:64], scalar=rz[:, bi, hp:hp + 1],
                                    in1=oacc[:, hp * 64:(hp + 1) * 64], op0=ALU.mult, op1=ALU.add), sig=(hp == 3))
                            if bi == 1:
                                ev_os_rd = evr
                            else:
                                ev_ow_rd = evr
                        ev_ob = ctx.op("dve", lambda e: e.tensor_copy(oab[:], oacc[:]), wait=[ev_tp_rd], sig=True)
                        ev_oacc_rd = ev_ob
                        for hf in range(2):
                            evt = ctx.op("pe", lambda e, hf=hf: e.transpose(TP[:, hf, :], oab[:, hf * 128:(hf + 1) * 128], G["ident"][:]),
                                         wait=[ev_ob, ev_tp_rd], sig=True)
                        ev_tp_rd = ctx.op("act", lambda e, g=g, qb=qb: e.activation(
                            out=oT[:, 2 * g:2 * g + 2, qb * 128:(qb + 1) * 128], in_=TP[:], func=AF.Copy), wait=[evt], sig=True)
                evd = None
                if "dbg" in G and b == 0:
                    evd = dbg_dump(ctx, G, [("oT0", oT[:, 0, :], SEQ), ("oT1", oT[:, 1, :], SEQ), ("rz", rz[:, :, :], 12),
                                            ("gat15", gat[:, 15, :], 12), ("imp", imp[:, :], 32), ("wk", wk[:, :], 32),
                                            ("m8", m8[:, :], 16), ("oacc", oacc[:, :], 256)],
                                   wait=[ev_tp_rd, ev_os_rd, ev_ow_rd])
                ctx.barrier([ev_tp_rd, evPT[0], evPT[1], ev_os_rd, ev_ow_rd, evd])
            ctx.stack = st

            for t in range(NQB):
                tg = b * NQB + t
                epi_load(ctx, G, E, tg)
                for m in range(8):
                    for f in range(8):
                        evy = ctx.op("pe", lambda e, m=m, f=f, t=t: e.matmul(
                            E.Y[m % 2][:], lhsT=Wno[:, f, m * 128:(m + 1) * 128], rhs=oT[:, f, t * 128:(t + 1) * 128],
                            start=(f == 0), stop=(f == 7)),
                            wait=[ev_tp_rd, E.ev_ybank[m % 2], evP[m % 2]] if f == 0 else (), sig=(f == 7))
                    epi_z(ctx, G, E, tg, L, s, m, evy)
                    evP[m % 2] = E.ev_ybank[m % 2]
                ctx.op("pe", None, wait=[evS[0], evS[1]])
                ev_st = epi_stats_pe(ctx, G, E, tg)
                last_store = epi_norm(ctx, G, E, tg, L, s, ev_st)
                evS[0] = E.ev_stats_rd
                evS[1] = E.ev_stats_rd
            ctx.barrier([last_store, E.ev_xfree[0], E.ev_xfree[1], ev_st])
    ctx.stack = outer

def build_program(phases):
    nc = bass.Bass("TRN2", target_bir_lowering=False)
    ins = {}

    def inp(name, shape):
        ins[name] = nc.dram_tensor(name, list(shape), F32, kind="ExternalInput").ap()
        return ins[name]

    xT_in = inp("xT_in", [D, TOK])
    cT = inp("cT", [128, 8, NB])
    ada_w = inp("ada_w", [DEPTH, D, 9 * D])
    ada_b = inp("ada_bT", [128, DEPTH, 72])
    lng = inp("lng", [128, DEPTH, 3, 8])
    lnb = inp("lnb", [128, DEPTH, 3, 8])
    need_ffn = any(k == "ffn" for (k, L_, s_) in phases)
    need_ret = any(k == "mix" and L_ % 2 == 0 for (k, L_, s_) in phases)
    need_nsa = any(k == "mix" and L_ % 2 == 1 for (k, L_, s_) in phases)
    if need_ffn:
        ffn_w_in = inp("ffn_w_in", [DEPTH, 2, D, 2 * DFF])
        ffn_w_out = inp("ffn_w_out", [DEPTH, 2, DFF, D])
    if need_ret:
        ret_w_in = inp("ret_w_in", [2, D, 6144])
        ret_w_out = inp("ret_w_out", [2, 2048, D])
    if need_nsa:
        NW = {}
        NW["w_in"] = inp("nsa_w_in", [2, D, 2608])
        NW["w_out"] = inp("nsa_w_out", [2, D, D])
        NW["ck_w1"] = inp("nsa_ck_w1", [2, 2048, 256])
        NW["ck_w2"] = inp("nsa_ck_w2", [2, 256, 64])
        NW["cv_w1"] = inp("nsa_cv_w1", [2, 2048, 256])
        NW["cv_w2"] = inp("nsa_cv_w2", [2, 256, 64])
        NW["pekT"] = inp("pekT", [2, 64, 32])
        NW["pevT"] = inp("pevT", [2, 64, 32])
        NT_ = {}
        for (nm_, shp) in (("c_E", [32, SEQ]), ("c_ones_row", [1, SEQ]), ("c_ones4", [1, 4, 128]), ("c_qshift", [16, SEQ]),
                           ("c_Bk", [128, 16, 16]), ("c_Bc", [128, 16]), ("c_cmask", [128, 16, 128]), ("c_tri", [128, 128]),
                           ("c_triU", [128, 128]), ("c_topk", [128, 16, 2, 32]), ("c_ovl", [128, 32])):
            NT_[nm_] = inp(nm_, shp)
    c_dq = inp("c_dq", [128, 4, 128])
    c_dk = inp("c_dk", [128, 4])
    c_DT = inp("c_DT", [128, 4, 128])
    c_ident = inp("c_ident", [128, 128])
    out = nc.dram_tensor("out", [D, TOK], F32, kind="ExternalOutput").ap()
    dbg = nc.dram_tensor("dbg", [128, 16384], F32, kind="ExternalOutput").ap() if os.environ.get("DBG") else None

    import contextlib
    with contextlib.ExitStack() as stack:
        ctx = Ctx(nc, stack)
        G = {}
        G["xT"] = out
        if dbg is not None:
            G["dbg"] = dbg
            G["dbg_layout"] = DBG_LAYOUT
            del DBG_LAYOUT[:]
        G["cT"] = sb(ctx, "cT_sb", [128, 8, NB], F32)
        G["modT"] = sb(ctx, "modT", [128, DEPTH, 72, NB], F32)
        G["A"] = sb(ctx, "Atab", [128, DEPTH, 3, NB, 8], F32)
        G["C"] = sb(ctx, "Ctab", [128, DEPTH, 3, NB, 8], F32)
        G["lng"] = sb(ctx, "lng_sb", [128, DEPTH, 3, 8], F32)
        G["lnb"] = sb(ctx, "lnb_sb", [128, DEPTH, 3, 8], F32)
        G["ones_bf"] = sb(ctx, "ones_bf", [128, 128], BF16)
        G["eps"] = sb(ctx, "eps_sb", [128, 1], F32)
        G["dqT"] = sb(ctx, "dqT_sb", [128, 4, 128], F32)
        G["dk"] = sb(ctx, "dk_sb", [128, 4], F32)
        G["DT"] = sb(ctx, "DT_sb", [128, 4, 128], F32)
        G["ident"] = sb(ctx, "ident_sb", [128, 128], BF16)
        G["gneps"] = sb(ctx, "gneps_sb", [128, 1], F32)
        s0 = ctx.sem("init")
        ctx.dma("sp", G["dqT"][:], c_dq, s0)
        ctx.dma("sp", G["dk"][:], c_dk, s0)
        ctx.dma("sp", G["DT"][:], c_DT, s0)
        s0p = ctx.sem("initp")
        ev0p = ctx.dma("pool", G["ident"][:], c_ident, s0p)
        ctx.op("dve", lambda e: e.memset(G["gneps"][:], 1e-6))
        ctx.dma("sp", G["lng"][:], lng, s0)
        ev0 = ctx.dma("sp", G["lnb"][:], lnb, s0)
        s1 = ctx.sem("xcopy")
        for m in range(8):
            ev1 = ctx.dma("act" if m % 2 else "sp", out[m * 128:(m + 1) * 128, :], xT_in[m * 128:(m + 1) * 128, :], s1)
        ctx.op("dve", lambda e: e.memset(G["eps"][:], EPS_P))
        ev2 = ctx.op("dve", lambda e: e.memset(G["ones_bf"][:], 1.0 / D), sig=True)
        ctx.barrier([ev0, ev1, ev2, ev0p])
        if not os.environ.get('SKIP_ADA'):
            ada_phase(ctx, G, cT, ada_w, ada_b)
        for ph in phases:
            kind, L, s = ph
            if kind == "ffn":
                ffn_phase(ctx, G, L, s, ffn_w_in[L, s // 2], ffn_w_out[L, s // 2])
            elif kind == "mix" and L % 2 == 0:
                ret_phase(ctx, G, L, ret_w_in[L // 2], ret_w_out[L // 2])
            elif kind == "mix" and L % 2 == 1:
                G.update(NT_)
                nsa_phase(ctx, G, L, {k_: v_[L // 2] for k_, v_ in NW.items()})
        ctx.emit()
    return nc, set(ins.keys())


def host_inputs(inputs, core):
    b0 = core * NB
    x = np.asarray(inputs["x"][b0:b0 + NB], np.float32).reshape(TOK, D)
    m = {}
    m["xT_in"] = np.ascontiguousarray(x.T)
    c = np.asarray(inputs["c"][b0:b0 + NB], np.float32)
    m["cT"] = np.ascontiguousarray(c.reshape(NB, 8, 128).transpose(2, 1, 0))
    m["ada_w"] = np.asarray(inputs["ada_w"], np.float32)
    m["ada_bT"] = np.ascontiguousarray(np.asarray(inputs["ada_b"], np.float32).reshape(DEPTH, 72, 128).transpose(2, 0, 1))
    m["lng"] = np.ascontiguousarray(np.asarray(inputs["ln_g"], np.float32).reshape(DEPTH, 3, 8, 128).transpose(3, 0, 1, 2))
    m["lnb"] = np.ascontiguousarray(np.asarray(inputs["ln_b"], np.float32).reshape(DEPTH, 3, 8, 128).transpose(3, 0, 1, 2))
    m["ffn_w_in"] = np.asarray(inputs["ffn_w_in"], np.float32)
    m["ffn_w_out"] = np.asarray(inputs["ffn_w_out"], np.float32)
    m["ret_w_in"] = np.asarray(inputs["ret_w_in"], np.float32)
    m["ret_w_out"] = np.asarray(inputs["ret_w_out"], np.float32)
    gam = np.array(GAM, np.float64)
    i = np.arange(128, dtype=np.float64)
    dq = gam[:, None] ** (i[None, :] + 1.0)
    m["c_dq"] = np.ascontiguousarray(np.broadcast_to(dq[None], (128, 4, 128))).astype(np.float32)
    m["c_dk"] = np.ascontiguousarray((gam[None, :] ** (127.0 - i[:, None])) * (256.0 ** -0.5)).astype(np.float32)
    diff = i[None, :] - i[:, None]
    DT = np.where(diff[None] >= 0, gam[:, None, None] ** np.maximum(diff[None], 0.0), 0.0) * (256.0 ** -0.5)
    m["c_DT"] = np.ascontiguousarray(DT.transpose(1, 0, 2)).astype(np.float32)
    m["c_ident"] = np.eye(128, dtype=np.float32)
    for k_ in ("nsa_w_in", "nsa_w_out", "nsa_ck_w1", "nsa_ck_w2", "nsa_cv_w1", "nsa_cv_w2"):
        m[k_] = np.asarray(inputs[k_], np.float32)
    m["pekT"] = np.ascontiguousarray(np.asarray(inputs["nsa_pe_k"], np.float32).transpose(0, 2, 1))
    m["pevT"] = np.ascontiguousarray(np.asarray(inputs["nsa_pe_v"], np.float32).transpose(0, 2, 1))
    m.update(nsa_tables())
    return m


_NSA_TABLES = {}


def nsa_tables():
    if _NSA_TABLES:
        return _NSA_TABLES
    T = _NSA_TABLES
    key = np.arange(SEQ)
    T["c_E"] = (key[None, :] // 64 == np.arange(32)[:, None]).astype(np.float32)
    T["c_ones_row"] = np.ones((1, SEQ), np.float32)
    T["c_ones4"] = np.ones((1, 4, 128), np.float32)
    sl = np.array(SLOPES, np.float64)
    T["c_qshift"] = (-sl[:, None] * key[None, :].astype(np.float64)).astype(np.float32)
    j = np.arange(128, dtype=np.float64)
    kt = np.arange(16, dtype=np.float64)
    T["c_Bk"] = (sl[None, :, None] * (kt[None, None, :] * 128.0 + j[:, None, None])).astype(np.float32)
    n = np.arange(128)
    cend = 16.0 * n + 31.0
    T["c_Bc"] = (sl[None, :] * cend[:, None]).astype(np.float32)
    t = (np.arange(16)[:, None] * 128 + np.arange(128)[None, :])
    cm = (t[None, :, :] >= cend[:, None, None]) & (n[:, None, None] < NCMP)
    T["c_cmask"] = cm.astype(np.float32)
    T["c_tri"] = (np.arange(128)[None, :] >= np.arange(128)[:, None]).astype(np.float32)
    T["c_triU"] = (np.arange(128)[:, None] > np.arange(128)[None, :]).astype(np.float32)
    tq = (np.arange(16)[None, :] * 128 + np.arange(128)[:, None])
    blk = np.arange(32)
    valid = blk[None, None, :] * 64 <= tq[:, :, None]
    cur = tq[:, :, None] // 64
    forced = (blk[None, None, :] == 0) | (blk[None, None, :] == cur) | (blk[None, None, :] == cur - 1)
    tk = np.zeros((128, 16, 2, 32), np.float32)
    tk[:, :, 0, :] = (valid & ~forced)
    tk[:, :, 1, :] = np.where(valid, np.where(forced, 1e4 + blk[None, None, :], 0.0), -1.0)
    T["c_topk"] = tk
    c0 = np.arange(NCMP) * 16
    s0 = np.arange(32) * 64
    lo = np.maximum(c0[:, None], s0[None, :])
    hi = np.minimum(c0[:, None] + 32, s0[None, :] + 64)
    ov = np.zeros((128, 32), np.float32)
    ov[:NCMP] = np.clip(hi - lo, 0, None) / 32.0
    T["c_ovl"] = ov
    return T


def all_phases():
    ph = []
    for L in range(DEPTH):
        ph.append(("ffn", L, 0))
        ph.append(("mix", L, 1))
        ph.append(("ffn", L, 2))
    return ph


def run(inputs, phases, cores=8):
    nc, names = build_program(phases)
    in_maps = [{k: v for k, v in host_inputs(inputs, c).items() if k in names} for c in range(cores)]
    res = run_bass_kernel_spmd(nc, in_maps, core_ids=list(range(cores)))
    outs = [np.asarray(r["out"]).T.reshape(NB, SEQ, D) for r in res.results]
    return np.concatenate(outs, axis=0)


def kernel(**inputs):
    return run(inputs, all_phases(), 8).astype(np.float32)
```

```python
import numpy as np
import concourse.bass as bass
import concourse.mybir as mybir
from concourse.bass_utils import run_bass_kernel_spmd

F32 = mybir.dt.float32
BF16 = mybir.dt.bfloat16
AF = mybir.ActivationFunctionType
ALU = mybir.AluOpType
AX = mybir.AxisListType

D = 1024
SEQ = 2048
NB = 2
TOK = NB * SEQ
DEPTH = 4
DFF = 2816
NJ = DFF // 128
NT = 256
import os
NG = int(os.environ.get('NGRUN', TOK // NT))
ALPHA = (2 * DEPTH) ** 0.25
LN_EPS = 1e-5
EPS_P = LN_EPS / (ALPHA * ALPHA)
FFN_RES = 0.5

ENGS = ("pe", "act", "dve", "pool", "sp")


class Sem:
    def __init__(self, nc, stack, name):
        self.h = stack.enter_context(nc.semaphore(name))
        self.v = 0


class Ctx:
    def __init__(self, nc, stack):
        self.nc = nc
        self.stack = stack
        self.semstack = stack
        self.sems = {}
        self.eng = {"pe": nc.tensor, "act": nc.scalar, "dve": nc.vector, "pool": nc.gpsimd, "sp": nc.sync}
        self.esem = {e: Sem(nc, stack, "s_" + e) for e in ENGS if e != "sp"}
        self.waited = {e: {} for e in ENGS}
        self.ninst = 0
        self.nsig = 0
        self.strict = True
        self.last_ev = {}

    def sem(self, name):
        if name not in self.sems:
            self.sems[name] = Sem(self.nc, self.semstack, name)
        return self.sems[name]

    def _waits(self, eng, wait):
        out = []
        for ev in wait:
            if ev is None:
                continue
            s, v = ev
            if v <= 0:
                continue
            if self.waited[eng].get(id(s), 0) >= v:
                continue
            self.waited[eng][id(s)] = v
            out.append((s.h, v))
        return out

    def op(self, eng, fn, wait=(), sig=False):
        e = self.eng[eng]
        strict = self.strict and eng == "dve" and fn is not None
        if strict:
            wait = list(wait) + [self.last_ev.get(eng)]
            sig = True
        for (h, v) in self._waits(eng, wait):
            e.wait_ge(h, v)
        if fn is None:
            return None
        ins = fn(e)
        self.ninst += 1
        if sig:
            s = self.esem[eng]
            s.v += 1
            ins.then_inc(s.h, 1)
            self.nsig += 1
            self.last_ev[eng] = (s, s.v)
            return (s, s.v)
        return None

    def dma(self, eng, out, in_, sem, wait=()):
        e = self.eng[eng]
        for (h, v) in self._waits(eng, wait):
            e.wait_ge(h, v)
        sem.v += 16
        e.dma_start(out=out, in_=in_).then_inc(sem.h, 16)
        self.ninst += 1
        return (sem, sem.v)

    def emit(self):
        pass

    def barrier(self, evs):
        for e in ENGS:
            self.op(e, None, wait=evs)


_UNIQ = [0]
DBG_LAYOUT = []


def _uname(name):
    _UNIQ[0] += 1
    return "%s_u%d" % (name, _UNIQ[0])


def dbg_dump(ctx, G, items, wait):
    if "dbg" not in G:
        return None
    sem = ctx.sem("dbgsem")
    col = G.setdefault("dbg_col", [0])
    ev = None
    for (name, ap, n) in items:
        ev = ctx.dma("pool", G["dbg"][:, col[0]:col[0] + n], ap, sem, wait=wait)
        G["dbg_layout"].append((name, col[0], n))
        col[0] += n
    return ev


def sb(ctx, name, shape, dt):
    return ctx.stack.enter_context(ctx.nc.sbuf_tensor(_uname(name), shape, dt))


def ps(ctx, name, shape, dt=F32):
    return ctx.stack.enter_context(ctx.nc.psum_tensor(_uname(name), shape, dt))


def ada_phase(ctx, G, cT, ada_w, ada_bT):
    nc = ctx.nc
    PIECE = 1152
    NP = 9216 // PIECE
    with ctx.nc.sbuf_tensor("ada_wbuf", [128, 2, 8, PIECE], BF16) as wbuf, \
            ctx.nc.sbuf_tensor("ada_ctb", [128, 8, NB], BF16) as ctb, \
            ctx.nc.sbuf_tensor("ada_bT_sb", [128, DEPTH, 72], F32) as bT, \
            ctx.nc.psum_tensor("ada_ps0", [128, 72, 2], F32) as mps0, \
            ctx.nc.psum_tensor("ada_ps1", [128, 72, 2], F32) as mps1:
        mps = [mps0, mps1]
        ct = G["cT"]
        s_in = ctx.sem("adain")
        s_w = [ctx.sem("adaw0"), ctx.sem("adaw1")]
        ctx.dma("sp", bT[:], ada_bT, s_in)
        ev_in = ctx.dma("sp", ct[:], cT, s_in)
        ev_c = ctx.op("act", lambda e: e.activation(out=ctb[:], in_=ct[:], func=AF.Silu), wait=[ev_in], sig=True)
        free = [None, None]
        ev_copy = [None, None]
        n = 0
        for L in range(DEPTH):
            wv = ada_w[L].rearrange("(k p) n -> p k n", p=128)
            last = None
            for pc in range(NP):
                sl = n % 2
                for k in range(8):
                    ev_ld = ctx.dma("pool", wbuf[:, sl, k, :], wv[:, k, pc * PIECE:(pc + 1) * PIECE], s_w[sl], wait=[free[sl]])
                for cc in range(PIECE // 128):
                    ch = pc * (PIECE // 128) + cc
                    for k in range(8):
                        last = ctx.op("pe", lambda e, sl=sl, k=k, cc=cc, ch=ch, L=L: e.matmul(
                            mps[L % 2][:, ch, :], lhsT=wbuf[:, sl, k, cc * 128:(cc + 1) * 128], rhs=ctb[:, k, :],
                            start=(k == 0), stop=(k == 7)),
                            wait=[ev_ld, ev_c, ev_copy[L % 2]] if (cc == 0 and k == 0) else (),
                            sig=(cc == PIECE // 128 - 1 and k == 7))
                free[sl] = last
                n += 1
            for b in range(NB):
                ev_copy[L % 2] = ctx.op("dve", lambda e, L=L, b=b: e.tensor_tensor(
                    G["modT"][:, L, :, b], mps[L % 2][:, :, b], bT[:, L, :], op=ALU.add),
                    wait=[last, ev_in], sig=(b == NB - 1))
        last = None
        for L in range(DEPTH):
            for s in range(3):
                for b in range(NB):
                    sc = G["modT"][:, L, (3 * s + 1) * 8:(3 * s + 1) * 8 + 8, b]
                    gt = G["modT"][:, L, (3 * s + 2) * 8:(3 * s + 2) * 8 + 8, b]
                    rw = (1.0 if s == 1 else FFN_RES) / ALPHA
                    ctx.op("dve", lambda e, L=L, s=s, b=b, sc=sc: e.tensor_scalar_add(G["A"][:, L, s, b, :], sc, 1.0),
                           wait=[ev_copy[0], ev_copy[1]])
                    last = ctx.op("dve", lambda e, L=L, s=s, b=b, gt=gt, rw=rw: e.tensor_scalar(
                        G["C"][:, L, s, b, :], gt, 1.0, rw, op0=ALU.add, op1=ALU.mult), sig=(L == DEPTH - 1 and s == 2 and b == NB - 1))
        ctx.barrier([last])
    return last


class Epi:
    def __init__(self, ctx, G, zb=None, zsq=None, nt=None, Y=None, ST=None):
        nt = nt or NT
        self.nt = nt
        self.xbuf = [sb(ctx, "xbuf%d" % i, [128, 8, nt], F32) for i in range(2)]
        self.zb = zb if zb is not None else sb(ctx, "zb", [128, 8, nt], BF16)
        self.zsq = zsq if zsq is not None else sb(ctx, "zsq", [128, 8, nt], BF16)
        self.mean = sb(ctx, "mean_sb", [128, nt], F32)
        self.var = sb(ctx, "var_sb", [128, nt], F32)
        self.rstd = sb(ctx, "rstd_sb", [128, nt], F32)
        self.tmp = sb(ctx, "tmp_sb", [128, nt], F32)
        if Y is not None:
            self.Y = Y
            self.ST = ST
        else:
            self.Y = [ps(ctx, "Yps%d" % i, [128, nt]) for i in range(2)]
            self.ST = [ps(ctx, "STps%d" % i, [128, nt]) for i in range(2)]
        self.s_x = [ctx.sem("xld0"), ctx.sem("xld1")]
        self.s_o = [ctx.sem("xst0"), ctx.sem("xst1")]
        self.ev_xfree = [None, None]
        self.ev_x = [None, None]
        self.ev_ybank = [None, None]
        self.ev_stats_rd = None
        self.ev_stats_pe = None
        self.ev_zact = None
        self.ev_stpe_done = None


def xT_view(G, t, nt):
    return G["xT"].rearrange("(m p) t -> p m t", p=128)[:, :, t * nt:(t + 1) * nt]


def epi_load(ctx, G, E, t):
    sl = t % 2
    E.ev_x[sl] = ctx.dma("sp", E.xbuf[sl][:], xT_view(G, t, E.nt), E.s_x[sl], wait=[E.ev_xfree[sl]])


def epi_h(ctx, G, E, t, L, s, hT, wait=()):
    sl = t % 2
    b = t // ((TOK // E.nt) // NB)
    ev = None
    for m in range(8):
        ev = ctx.op("dve", lambda e, m=m: e.tensor_scalar(
            hT[:, m, :], E.xbuf[sl][:, m, :], G["A"][:, L, s, b, m:m + 1],
            G["modT"][:, L, (3 * s) * 8 + m, b:b + 1], op0=ALU.mult, op1=ALU.add),
            wait=[E.ev_x[sl]] + list(wait) if m == 0 else (), sig=(m == 7))
    return ev


def epi_z(ctx, G, E, t, L, s, m, ev_y):
    sl = t % 2
    b = t // ((TOK // E.nt) // NB)
    evz = ctx.op("dve", lambda e: e.scalar_tensor_tensor(
        out=E.xbuf[sl][:, m, :], in0=E.Y[m % 2][:], scalar=G["C"][:, L, s, b, m:m + 1], in1=E.xbuf[sl][:, m, :],
        op0=ALU.mult, op1=ALU.add), wait=[ev_y, E.ev_x[sl]], sig=True)
    E.ev_ybank[m % 2] = evz
    ctx.op("act", lambda e: e.activation(out=E.zb[:, m, :], in_=E.xbuf[sl][:, m, :], func=AF.Copy),
           wait=[evz, E.ev_stpe_done])
    E.ev_zact = ctx.op("act", lambda e: e.activation(out=E.zsq[:, m, :], in_=E.xbuf[sl][:, m, :], func=AF.Square),
                       sig=(m == 7))


def epi_stats_pe(ctx, G, E, t):
    for i, src in enumerate((E.zb, E.zsq)):
        for m in range(8):
            ev = ctx.op("pe", lambda e, i=i, m=m, src=src: e.matmul(
                E.ST[i][:], lhsT=G["ones_bf"][:], rhs=src[:, m, :], start=(m == 0), stop=(m == 7)),
                wait=[E.ev_zact, E.ev_stats_rd] if (i == 0 and m == 0) else (), sig=(i == 1 and m == 7))
    E.ev_stpe_done = ev
    return ev


def epi_norm(ctx, G, E, t, L, s, ev_st):
    sl = t % 2
    ctx.op("dve", lambda e: e.tensor_copy(E.mean[:], E.ST[0][:]), wait=[ev_st])
    ctx.op("dve", lambda e: e.tensor_tensor(E.tmp[:], E.ST[0][:], E.mean[:], op=ALU.mult))
    E.ev_stats_rd = ctx.op("dve", lambda e: e.tensor_tensor(E.var[:], E.ST[1][:], E.tmp[:], op=ALU.subtract), sig=True)
    ev_sd = ctx.op("act", lambda e: e.activation(out=E.rstd[:], in_=E.var[:], func=AF.Sqrt, bias=G["eps"][:, 0:1], scale=1.0),
                   wait=[E.ev_stats_rd], sig=True)
    ctx.op("dve", lambda e: e.reciprocal(E.rstd[:], E.rstd[:]), wait=[ev_sd])
    evn = None
    for m in range(8):
        ctx.op("dve", lambda e, m=m: e.tensor_tensor(E.tmp[:], E.xbuf[sl][:, m, :], E.mean[:], op=ALU.subtract))
        evn = ctx.op("dve", lambda e, m=m: e.tensor_tensor(E.xbuf[sl][:, m, :], E.tmp[:], E.rstd[:], op=ALU.mult),
                     sig=(m == 7))
    eva = None
    for m in range(8):
        eva = ctx.op("act", lambda e, m=m: e.activation(
            out=E.xbuf[sl][:, m, :], in_=E.xbuf[sl][:, m, :], func=AF.Identity,
            scale=G["lng"][:, L, s, m:m + 1], bias=G["lnb"][:, L, s, m:m + 1]),
            wait=[evn] if m == 0 else (), sig=(m == 7))
    E.ev_xfree[sl] = ctx.dma("sp", xT_view(G, t, E.nt), E.xbuf[sl][:], E.s_o[sl], wait=[eva])
    return E.ev_xfree[sl]


def ffn_phase(ctx, G, L, s, w_in, w_out):
    nc = ctx.nc
    import contextlib
    outer = ctx.stack
    ctx.strict = False
    with contextlib.ExitStack() as st:
        ctx.stack = st
        Win = sb(ctx, "Win", [128, 8, 2 * DFF], BF16)
        Wout = sb(ctx, "Wout", [128, NJ, D], BF16)
        hT = sb(ctx, "hT", [128, 8, NT], BF16)
        gT = sb(ctx, "gT", [128, NJ, NT], BF16)
        sa = [sb(ctx, "sa%d" % i, [128, NT], F32) for i in range(2)]
        A = [ps(ctx, "Aps%d" % i, [128, NT]) for i in range(2)]
        U = [ps(ctx, "Ups%d" % i, [128, NT]) for i in range(2)]
        E = Epi(ctx, G)
        s_w = ctx.sem("ffnw")
        wv = w_in.rearrange("(k p) n -> p k n", p=128)
        for k in range(8):
            ev_w = ctx.dma("pool", Win[:, k, :], wv[:, k, :], s_w)
        wo = w_out.rearrange("(j p) n -> p j n", p=128)
        for j0 in range(0, NJ, 11):
            ev_w = ctx.dma("pool", Wout[:, j0:j0 + 11, :], wo[:, j0:j0 + 11, :], s_w)

        ev_au = [None, None]
        ev_silu = [None, None]
        ev_g = [None, None]
        ev_hdone = [None]
        ev_glast = [None]
        ev_aulast = [None]
        ev_ylast = [None]

        def au(t):
            for j in range(NJ):
                bk = j % 2
                for k in range(8):
                    ctx.op("pe", lambda e, j=j, k=k, bk=bk: e.matmul(
                        A[bk][:], lhsT=Win[:, k, j * 128:(j + 1) * 128], rhs=hT[:, k, :], start=(k == 0), stop=(k == 7)),
                        wait=[ev_w, ev_hdone[0], ev_silu[bk]] if k == 0 else ())
                for k in range(8):
                    evp = ctx.op("pe", lambda e, j=j, k=k, bk=bk: e.matmul(
                        U[bk][:], lhsT=Win[:, k, DFF + j * 128:DFF + (j + 1) * 128], rhs=hT[:, k, :],
                        start=(k == 0), stop=(k == 7)),
                        wait=[ev_g[bk]] if k == 0 else (), sig=(k == 7))
                ev_silu[bk] = ctx.op("act", lambda e, bk=bk: e.activation(out=sa[bk][:], in_=A[bk][:], func=AF.Silu),
                                     wait=[evp, ev_g[bk]], sig=True)
                ev_g[bk] = ctx.op("dve", lambda e, j=j, bk=bk: e.tensor_tensor(
                    gT[:, j, :], sa[bk][:], U[bk][:], op=ALU.mult), wait=[ev_silu[bk], ev_ylast[0]], sig=True)
            ev_aulast[0] = evp
            ev_glast[0] = ev_g[(NJ - 1) % 2]

        def y(t):
            for m in range(8):
                for j in range(NJ):
                    evy = ctx.op("pe", lambda e, m=m, j=j: e.matmul(
                        E.Y[m % 2][:], lhsT=Wout[:, j, m * 128:(m + 1) * 128], rhs=gT[:, j, :],
                        start=(j == 0), stop=(j == NJ - 1)),
                        wait=[ev_glast[0], E.ev_ybank[m % 2]] if j == 0 else (), sig=(j == NJ - 1))
                epi_z(ctx, G, E, t, L, s, m, evy)
            ev_ylast[0] = evy

        epi_load(ctx, G, E, 0)
        epi_load(ctx, G, E, 1)
        ev_hdone[0] = epi_h(ctx, G, E, 0, L, s, hT)
        last = None
        for t in range(NG):
            au(t)
            if t > 0:
                ev_st = epi_stats_pe(ctx, G, E, t - 1)
                last = epi_norm(ctx, G, E, t - 1, L, s, ev_st)
                if t + 1 < NG:
                    epi_load(ctx, G, E, t + 1)
            y(t)
            if t + 1 < NG:
                ev_hdone[0] = epi_h(ctx, G, E, t + 1, L, s, hT, wait=[ev_aulast[0]])
        ev_st = epi_stats_pe(ctx, G, E, NG - 1)
        last2 = epi_norm(ctx, G, E, NG - 1, L, s, ev_st)
        ctx.barrier([last, last2, ev_st])
    ctx.stack = outer
    ctx.strict = True


RH = 4
GAM = [1.0 - 2.0 ** (-5.0 - h) for h in range(RH)]
NT_R = 128
NG_R = int(os.environ.get('NGRUN', TOK // NT_R))
NCH_R = NT_R // 128


def ret_phase(ctx, G, L, w_in, w_out):
    import contextlib
    s = 1
    outer = ctx.stack
    with contextlib.ExitStack() as st:
        ctx.stack = st
        Wr = sb(ctx, "Wr", [128, 8, 6144], BF16)
        Wo = sb(ctx, "Wo", [128, 16, D], BF16)
        hT = sb(ctx, "hTr", [128, 8, NT_R], BF16)
        og = sb(ctx, "og", [128, 2048], BF16)
        qT = sb(ctx, "qT", [128, 8, NT_R], BF16)
        q2T = sb(ctx, "q2T", [128, 8, NT_R], BF16)
        kT = sb(ctx, "kT", [128, 8, NT_R], BF16)
        k2 = sb(ctx, "k2", [128, 1, 1024], BF16)
        vt = sb(ctx, "vt", [128, 1, 2048], BF16)
        sg = sb(ctx, "sg", [128, 1, 2048], BF16)
        on = sb(ctx, "on", [128, 512], F32)
        ogT = sb(ctx, "ogT", [128, 16, NT_R], BF16)
        S = sb(ctx, "S", [128, 8, 512], F32)
        Sb = sb(ctx, "Sb", [128, 8, 512], BF16)
        sTb = sb(ctx, "sTb", [128, 128], BF16)
        st = sb(ctx, "gnst", [128, 8], F32)
        P = [ps(ctx, "Pps%d" % i, [128, 512]) for i in range(2)]
        O = [ps(ctx, "Ops%d" % i, [128, 512]) for i in range(2)]
        SP = ps(ctx, "sTps", [128, 128])
        TP = ps(ctx, "TPps", [128, 4, 128], BF16)
        E = Epi(ctx, G, zb=qT, zsq=q2T, nt=NT_R, Y=[P[0][:, 0:NT_R], P[1][:, 0:NT_R]], ST=[O[0][:, 0:NT_R], O[1][:, 0:NT_R]])
        s_w = ctx.sem("retw")
        wv = w_in.rearrange("(k p) n -> p k n", p=128)
        for k in range(8):
            ev_w = ctx.dma("pool", Wr[:, k, :], wv[:, k, :], s_w)
        wo = w_out.rearrange("(j p) n -> p j n", p=128)
        for j0 in range(0, 16, 8):
            ev_w = ctx.dma("pool", Wo[:, j0:j0 + 8, :], wo[:, j0:j0 + 8, :], s_w)

        evP = [None, None]
        pidx = [0]
        last_store = None
        epi_load(ctx, G, E, 0)
        if NG_R > 1:
            epi_load(ctx, G, E, 1)
        ev_prev_y = None
        ev_sb = None
        ev_Ofree = [None, None]
        for t in range(NG_R):
            sl = t % 2
            first_in_seq = (t % ((TOK // NT_R) // NB) == 0)
            ev_h = epi_h(ctx, G, E, t, L, s, hT, wait=[ev_prev_y])
            ev_s0 = None

            def proj_fm(dst_list, col0):
                evs = None
                for oc in range(8):
                    bk = pidx[0] % 2
                    pidx[0] += 1
                    for k in range(8):
                        evp = ctx.op("pe", lambda e, oc=oc, k=k, bk=bk: e.matmul(
                            P[bk][:, 0:NT_R], lhsT=Wr[:, k, col0 + oc * 128:col0 + (oc + 1) * 128], rhs=hT[:, k, :],
                            start=(k == 0), stop=(k == 7)), wait=[ev_w, ev_h, evP[bk]] if k == 0 else (), sig=(k == 7))
                    for (eng, fn) in dst_list:
                        evP[bk] = ctx.op(eng, lambda e, fn=fn, oc=oc, bk=bk: fn(e, oc, P[bk][:, 0:NT_R]), wait=[evp], sig=True)
                        evs = evP[bk]
                    if len(dst_list) == 2:
                        evP[bk] = evs
                return evs

            def q_act(e, oc, src):
                return e.activation(out=qT[:, oc, :], in_=src, func=AF.Copy)

            def q_dve(e, oc, src):
                return e.tensor_tensor(q2T[:, oc, :], src, G["dqT"][:, oc // 2, :], op=ALU.mult)

            def k_act(e, oc, src):
                return e.activation(out=kT[:, oc, :], in_=src, func=AF.Copy)

            ev_qa = None
            for oc in range(8):
                bk = pidx[0] % 2
                pidx[0] += 1
                for k in range(8):
                    evp = ctx.op("pe", lambda e, oc=oc, k=k, bk=bk: e.matmul(
                        P[bk][:, 0:NT_R], lhsT=Wr[:, k, oc * 128:(oc + 1) * 128], rhs=hT[:, k, :],
                        start=(k == 0), stop=(k == 7)), wait=[ev_w, ev_h, evP[bk]] if k == 0 else (), sig=(k == 7))
                ev_a = ctx.op("act", lambda e, oc=oc, bk=bk: q_act(e, oc, P[bk][:, 0:NT_R]), wait=[evp], sig=True)
                for c in range(NCH_R):
                    evP[bk] = ctx.op("dve", lambda e, oc=oc, bk=bk, c=c: e.tensor_tensor(
                        q2T[:, oc, c * 128:(c + 1) * 128], P[bk][:, c * 128:(c + 1) * 128], G["dqT"][:, oc // 2, :], op=ALU.mult),
                        wait=[evp, ev_a], sig=(c == NCH_R - 1))
            ev_q = evP[(pidx[0] - 1) % 2]
            ev_k = proj_fm([("act", k_act)], 1024)

            if ev_s0 is not None:
                ev_sb = ev_s0
            ev_og_free = None
            for c in range(NCH_R):
                cs = slice(c * 128, (c + 1) * 128)
                for nchunk in range(2 + 4 + 4):
                    bk = pidx[0] % 2
                    pidx[0] += 1
                    if nchunk < 2:
                        col0 = 1024 + nchunk * 512
                    elif nchunk < 6:
                        col0 = 2048 + (nchunk - 2) * 512
                    else:
                        col0 = 4096 + (nchunk - 6) * 512
                    for k in range(8):
                        evp = ctx.op("pe", lambda e, c=c, k=k, bk=bk, col0=col0: e.matmul(
                            P[bk][:], lhsT=hT[:, k, c * 128:(c + 1) * 128], rhs=Wr[:, k, col0:col0 + 512],
                            start=(k == 0), stop=(k == 7)), wait=[evP[bk]] if k == 0 else (), sig=(k == 7))
                    if nchunk < 2:
                        for hh in range(2):
                            h = nchunk * 2 + hh
                            evP[bk] = ctx.op("dve", lambda e, c=c, h=h, hh=hh, bk=bk: e.tensor_scalar(
                                k2[:, 0, h * 256:(h + 1) * 256], P[bk][:, hh * 256:(hh + 1) * 256],
                                G["dk"][:, h:h + 1], None, op0=ALU.mult), wait=[evp], sig=(hh == 1))
                        ev_k2 = evP[bk]
                    elif nchunk < 6:
                        evP[bk] = ctx.op("act", lambda e, c=c, n=nchunk - 2, bk=bk: e.activation(
                            out=vt[:, 0, n * 512:(n + 1) * 512], in_=P[bk][:], func=AF.Copy), wait=[evp], sig=True)
                    else:
                        evP[bk] = ctx.op("act", lambda e, c=c, n=nchunk - 6, bk=bk: e.activation(
                            out=sg[:, 0, n * 512:(n + 1) * 512], in_=P[bk][:], func=AF.Silu), wait=[evp], sig=True)
                    ev_tok = evP[bk]


                for hp in range(2):
                    ev_o = [None, None]
                    for hh in range(2):
                        h = hp * 2 + hh
                        for dd in range(2):
                            evs = ctx.op("pe", lambda e, h=h, dd=dd, cs=cs: e.matmul(
                                SP[:], lhsT=kT[:, 2 * h + dd, cs], rhs=qT[:, 2 * h + dd, cs], start=(dd == 0), stop=(dd == 1)),
                                wait=[ev_q, ev_k] if dd == 0 else (), sig=(dd == 1))
                        ev_st = ctx.op("dve", lambda e, h=h: e.tensor_tensor(sTb[:], SP[:], G["DT"][:, h, :], op=ALU.mult),
                                       wait=[evs], sig=True)
                        fresh = first_in_seq and c == 0
                        ev_o[hh] = ctx.op("pe", lambda e, h=h, hh=hh, c=c, fresh=fresh: e.matmul(
                            O[hh][:], lhsT=sTb[:], rhs=vt[:, 0, h * 512:(h + 1) * 512], start=True, stop=fresh),
                            wait=[ev_st, ev_sb, ev_og_free, ev_Ofree[hh], ev_tok, ev_k2, E.ev_stats_rd], sig=fresh)
                        for dd in range(2):
                            if fresh:
                                break
                            ev_o[hh] = ctx.op("pe", lambda e, h=h, hh=hh, dd=dd, cs=cs: e.matmul(
                                O[hh][:], lhsT=q2T[:, 2 * h + dd, cs], rhs=Sb[:, 2 * h + dd, :], start=False, stop=(dd == 1)),
                                sig=(dd == 1))
                        for dd in range(2):
                            bk = pidx[0] % 2
                            pidx[0] += 1
                            evp = ctx.op("pe", lambda e, h=h, dd=dd, c=c, bk=bk: e.matmul(
                                P[bk][:], lhsT=k2[:, 0, h * 256 + dd * 128:h * 256 + (dd + 1) * 128],
                                rhs=vt[:, 0, h * 512:(h + 1) * 512], start=True, stop=True), wait=[evP[bk]], sig=True)
                            if fresh:
                                evP[bk] = ctx.op("dve", lambda e, h=h, dd=dd, bk=bk: e.tensor_copy(
                                    S[:, 2 * h + dd, :], P[bk][:]), wait=[evp], sig=True)
                            else:
                                evP[bk] = ctx.op("dve", lambda e, h=h, dd=dd, bk=bk: e.scalar_tensor_tensor(
                                    out=S[:, 2 * h + dd, :], in0=S[:, 2 * h + dd, :], scalar=GAM[h] ** 128, in1=P[bk][:],
                                    op0=ALU.mult, op1=ALU.add), wait=[evp], sig=True)
                            ev_sb = ctx.op("act", lambda e, h=h, dd=dd: e.activation(
                                out=Sb[:, 2 * h + dd, :], in_=S[:, 2 * h + dd, :], func=AF.Copy), wait=[evP[bk], evp], sig=True)
                        ev_of = ctx.op("dve", lambda e, hh=hh: e.tensor_scalar(
                            on[:], O[hh][:], 1.0, 0.0, op0=ALU.mult, op1=ALU.add, accum_out=st[:, 0:1]),
                            wait=[ev_o[hh]], sig=True)
                        ev_acc = ctx.op("dve", lambda e, h=h: e.scalar_tensor_tensor(
                            out=og[:, h * 512:(h + 1) * 512], in0=on[:], scalar=1.0, in1=on[:],
                            op0=ALU.mult, op1=ALU.mult, accum_out=st[:, 1:2]), sig=True)
                        ctx.op("dve", lambda e: e.tensor_scalar(st[:, 2:4], st[:, 0:2], 1.0 / 512.0, None, op0=ALU.mult),
                               wait=[ev_acc])
                        ctx.op("dve", lambda e: e.tensor_tensor(st[:, 4:5], st[:, 2:3], st[:, 2:3], op=ALU.mult))
                        ev_var = ctx.op("dve", lambda e: e.tensor_tensor(st[:, 5:6], st[:, 3:4], st[:, 4:5], op=ALU.subtract), sig=True)
                        ev_sd = ctx.op("act", lambda e: e.activation(out=st[:, 6:7], in_=st[:, 5:6], func=AF.Sqrt,
                                                                      bias=G["gneps"][:, 0:1], scale=1.0), wait=[ev_var], sig=True)
                        ctx.op("dve", lambda e: e.reciprocal(st[:, 6:7], st[:, 6:7]), wait=[ev_sd])
                        ctx.op("dve", lambda e: e.tensor_scalar(
                            on[:], on[:], st[:, 2:3], st[:, 6:7], op0=ALU.subtract, op1=ALU.mult))
                        ev_og = ctx.op("dve", lambda e, h=h, c=c: e.tensor_tensor(
                            og[:, h * 512:(h + 1) * 512], on[:], sg[:, 0, h * 512:(h + 1) * 512], op=ALU.mult), sig=True)
                        ev_Ofree[hh] = ev_of
                ev_tr = None
                for f4 in range(4):
                    for ff in range(4):
                        f = f4 * 4 + ff
                        evt = ctx.op("pe", lambda e, f=f, ff=ff: e.transpose(TP[:, ff, :], og[:, f * 128:(f + 1) * 128], G["ident"][:]),
                                     wait=[ev_og, ev_tr] if ff == 0 else (), sig=(ff == 3))
                    ev_tr = ctx.op("act", lambda e, f4=f4, cs=cs: e.activation(
                        out=ogT[:, f4 * 4:(f4 + 1) * 4, cs], in_=TP[:], func=AF.Copy), wait=[evt], sig=True)
                ev_og_free = ev_tr
            for m in range(8):
                for f in range(16):
                    evy = ctx.op("pe", lambda e, m=m, f=f: e.matmul(
                        E.Y[m % 2][:], lhsT=Wo[:, f, m * 128:(m + 1) * 128], rhs=ogT[:, f, :],
                        start=(f == 0), stop=(f == 15)),
                        wait=[ev_tr, E.ev_ybank[m % 2], evP[m % 2]] if f == 0 else (), sig=(f == 15))
                epi_z(ctx, G, E, t, L, s, m, evy)
                evP[m % 2] = E.ev_ybank[m % 2]
            ev_prev_y = evy
            ctx.op("pe", None, wait=[ev_og_free, ev_og])
            ev_st = epi_stats_pe(ctx, G, E, t)
            last_store = epi_norm(ctx, G, E, t, L, s, ev_st)
            if t + 2 < NG_R:
                epi_load(ctx, G, E, t + 2)
        evd = None
        if "dbg" in G:
            evd = dbg_dump(ctx, G, [("hT", hT[:, 0, :], 128), ("kT0", kT[:, 0, :], 128), ("k2", k2[:, 0, :], 1024),
                                    ("vt", vt[:, 0, :], 2048), ("sg", sg[:, 0, :], 2048), ("og", og[:, :], 2048),
                                    ("on", on[:, :], 512), ("ogT0", ogT[:, 0, :], 128), ("S0", S[:, 0, :], 512),
                                    ("Sb0", Sb[:, 0, :], 512), ("sTb", sTb[:, :], 128), ("st", st[:, :], 8),
                                    ("mean", E.mean[:, :], 128), ("var", E.var[:, :], 128), ("rstd", E.rstd[:, :], 128),
                                    ("xb0", E.xbuf[0][:, 0, :], 128), ("xb7", E.xbuf[0][:, 7, :], 128),
                                    ("zb0", E.zb[:, 0, :], 128), ("zsq0", E.zsq[:, 0, :], 128), ("zb7", E.zb[:, 7, :], 128),
                                    ("ogT15", ogT[:, 15, :], 128)],
                           wait=[last_store, ev_st])
        ctx.barrier([last_store, E.ev_xfree[0], E.ev_xfree[1], ev_st, evd])
    ctx.stack = outer


NHEAD = 16
HPG = 4
DH = 64
NCMP = 127
KA = 97
SLOPES = [2.0 ** (-8.0 * (i + 1) / 16.0) for i in range(NHEAD)]
NQB = SEQ // 128
CLAMP = 40.0
NEGM = 30000.0


def nsa_phase(ctx, G, L, W):
    import contextlib
    s = 1
    nt = 128
    outer = ctx.stack
    ctx.strict = True
    with contextlib.ExitStack() as st:
        ctx.stack = st
        Wn = sb(ctx, "Wn", [128, 8, 2608], BF16)
        Wno = sb(ctx, "Wno", [128, 8, D], BF16)
        hT = sb(ctx, "hTn", [128, 8, SEQ], BF16)
        oT = sb(ctx, "oTn", [128, 8, SEQ], BF16)
        W2k = sb(ctx, "W2k", [128, 2, DH], BF16)
        W2v = sb(ctx, "W2v", [128, 2, DH], BF16)
        peT = sb(ctx, "peT", [64, 2, 32], BF16)
        kcA = sb(ctx, "kcA", [KA, 4, 128], BF16)
        vcA = sb(ctx, "vcA", [128, 4, 97], BF16)
        Bk = sb(ctx, "Bk", [128, NHEAD, NQB], F32)
        Bc = sb(ctx, "Bc", [128, NHEAD], F32)
        cmask = sb(ctx, "cmask", [128, NQB, 128], BF16)
        tri = sb(ctx, "tri", [128, 128], BF16)
        triU = sb(ctx, "triU", [128, 128], BF16)
        topk = sb(ctx, "topk", [128, NQB, 2, 32], F32)
        P = [ps(ctx, "nP%d" % i, [128, 512]) for i in range(2)]
        SPS = [ps(ctx, "nS%d" % i, [128, 128]) for i in range(2)]
        OC = ps(ctx, "nOC", [128, 4, 97])
        OS = ps(ctx, "nOS", [128, 4, 65])
        OW = ps(ctx, "nOW", [128, 4, 65])
        TP = ps(ctx, "nTP", [128, 2, 128], BF16)
        E = Epi(ctx, G, nt=nt, Y=[P[0][:, 0:nt], P[1][:, 0:nt]], ST=[SPS[0][:, 0:nt], SPS[1][:, 0:nt]])

        s_w = ctx.sem("nsaw")
        s_t = ctx.sem("nsat")
        wv = W["w_in"].rearrange("(k p) n -> p k n", p=128)
        for k in range(8):
            ctx.dma("pool", Wn[:, k, :], wv[:, k, :], s_w)
        ctx.dma("pool", Wno[:], W["w_out"].rearrange("(k p) n -> p k n", p=128), s_w)
        ctx.dma("pool", W2k[:], W["ck_w2"].rearrange("(c p) d -> p c d", p=128), s_w)
        ctx.dma("pool", W2v[:], W["cv_w2"].rearrange("(c p) d -> p c d", p=128), s_w)
        ctx.dma("pool", peT[:, 0, :], W["pekT"], s_w)
        ctx.dma("pool", peT[:, 1, :], W["pevT"], s_w)
        ctx.dma("pool", cmask[:], G["c_cmask"], s_w)
        ctx.dma("pool", tri[:], G["c_tri"], s_w)
        ctx.dma("pool", triU[:], G["c_triU"], s_w)
        for g in range(4):
            ctx.dma("pool", vcA[:, g, 64:96], G["c_ovl"], s_w)
        ev_w = ctx.dma("pool", kcA[96:97, :, :], G["c_ones4"], s_w)
        ctx.dma("sp", Bk[:], G["c_Bk"], s_t)
        ctx.dma("sp", Bc[:], G["c_Bc"], s_t)
        ev_t = ctx.dma("sp", topk[:], G["c_topk"], s_t)
        ctx.op("dve", lambda e: e.memset(kcA[64:96, :, :], 0.0))
        ev_ms = ctx.op("dve", lambda e: e.memset(vcA[:, :, 96:97], 1.0), sig=True)
        ctx.barrier([ev_w, ev_t, ev_ms])

        pidx = [0]
        evP = [None, None]
        sidx = [0]
        evS = [None, None]

        def pbank():
            bk = pidx[0] % 2
            pidx[0] += 1
            return bk

        for b in range(int(os.environ.get('NSA_DBG_B', NB))):
            ev_h = None
            for t in range(NQB):
                tg = b * NQB + t
                sl = tg % 2
                epi_load(ctx, G, E, tg)
                for m in range(8):
                    ev_h = ctx.op("dve", lambda e, m=m, sl=sl, t=t: e.tensor_scalar(
                        hT[:, m, t * 128:(t + 1) * 128], E.xbuf[sl][:, m, :], G["A"][:, L, s, b, m:m + 1],
                        G["modT"][:, L, (3 * s) * 8 + m, b:b + 1], op0=ALU.mult, op1=ALU.add),
                        wait=[E.ev_x[sl]] if m == 0 else (), sig=(m == 7))
                E.ev_xfree[sl] = ev_h

            with contextlib.ExitStack() as st1:
                ctx.stack = st1
                W1 = [sb(ctx, "W1k", [64, 32, 256], BF16), sb(ctx, "W1v", [64, 32, 256], BF16)]
                X = [sb(ctx, "kcX", [64, SEQ], BF16), sb(ctx, "vcX", [64, SEQ], BF16)]
                hid = sb(ctx, "hid", [128, 2, 2, 128], BF16)
                bkv = sb(ctx, "bkv", [128, 2, 2], F32)
                u = sb(ctx, "gu", [128, 128], F32)
                w_ = sb(ctx, "gw", [128, 128], F32)
                s_w1 = ctx.sem("nsaw1")
                ctx.dma("pool", W1[0][:], W["ck_w1"].rearrange("(l d) c -> d l c", d=64), s_w1)
                ev_w1 = ctx.dma("pool", W1[1][:], W["cv_w1"].rearrange("(l d) c -> d l c", d=64), s_w1)
                ev_b = None
                for kv in range(2):
                    for cc in range(2):
                        bk = pbank()
                        for l in range(32):
                            evp = ctx.op("pe", lambda e, kv=kv, cc=cc, l=l, bk=bk: e.matmul(
                                P[bk][:, 0:1], lhsT=W1[kv][:, l, cc * 128:(cc + 1) * 128], rhs=peT[:, kv, l:l + 1],
                                start=(l == 0), stop=(l == 31)), wait=[ev_w1, evP[bk]] if l == 0 else (), sig=(l == 31))
                        evP[bk] = ctx.op("dve", lambda e, kv=kv, cc=cc, bk=bk: e.tensor_copy(bkv[:, kv, cc:cc + 1], P[bk][:, 0:1]),
                                         wait=[evp], sig=True)
                        ev_b = evP[bk]
                for g in range(4):
                    for kv in range(2):
                        col = D + kv * 256 + g * 64
                        for sl4 in range(4):
                            bk = pbank()
                            for k in range(8):
                                evp = ctx.op("pe", lambda e, k=k, bk=bk, col=col, sl4=sl4: e.matmul(
                                    P[bk][0:64, :], lhsT=Wn[:, k, col:col + 64], rhs=hT[:, k, sl4 * 512:(sl4 + 1) * 512],
                                    start=(k == 0), stop=(k == 7)), wait=[ev_h, evP[bk]] if k == 0 else (), sig=(k == 7))
                            evP[bk] = ctx.op("act", lambda e, kv=kv, bk=bk, sl4=sl4: e.activation(
                                out=X[kv][:, sl4 * 512:(sl4 + 1) * 512], in_=P[bk][0:64, :], func=AF.Copy), wait=[evp], sig=True)
                    ev_x = evP[(pidx[0] - 1) % 2]
                    for kv in range(2):
                        for cc in range(2):
                            bk = pbank()
                            for l in range(32):
                                evp = ctx.op("pe", lambda e, kv=kv, cc=cc, l=l, bk=bk: e.matmul(
                                    P[bk][:, 0:NCMP], lhsT=W1[kv][:, l, cc * 128:(cc + 1) * 128],
                                    rhs=X[kv][:, l:l + 16 * (NCMP - 1) + 1:16], start=(l == 0), stop=(l == 31)),
                                    wait=[ev_x, evP[bk]] if l == 0 else (), sig=(l == 31))
                            evP[bk] = ctx.op("dve", lambda e, kv=kv, cc=cc, bk=bk: e.tensor_scalar(
                                u[:, 0:NCMP], P[bk][:, 0:NCMP], bkv[:, kv, cc:cc + 1], None, op0=ALU.add), wait=[evp, ev_b], sig=True)
                            ctx.op("dve", lambda e: e.tensor_tensor(w_[:, 0:NCMP], u[:, 0:NCMP], u[:, 0:NCMP], op=ALU.mult))
                            ctx.op("dve", lambda e: e.tensor_scalar(w_[:, 0:NCMP], w_[:, 0:NCMP], 0.044715, 1.0, op0=ALU.mult, op1=ALU.add))
                            ev1 = ctx.op("dve", lambda e: e.tensor_tensor(w_[:, 0:NCMP], w_[:, 0:NCMP], u[:, 0:NCMP], op=ALU.mult), sig=True)
                            ev2 = ctx.op("act", lambda e: e.activation(out=w_[:, 0:NCMP], in_=w_[:, 0:NCMP], func=AF.Tanh,
                                                                     scale=0.7978845608028654), wait=[ev1], sig=True)
                            ctx.op("dve", lambda e: e.tensor_scalar(w_[:, 0:NCMP], w_[:, 0:NCMP], 1.0, 0.5, op0=ALU.add, op1=ALU.mult),
                                   wait=[ev2])
                            ev_hid = ctx.op("dve", lambda e, kv=kv, cc=cc: e.tensor_tensor(
                                hid[:, kv, cc, 0:NCMP], w_[:, 0:NCMP], u[:, 0:NCMP], op=ALU.mult), sig=True)
                    bk = pbank()
                    for cc in range(2):
                        evp = ctx.op("pe", lambda e, cc=cc, bk=bk: e.matmul(
                            P[bk][0:64, 0:NCMP], lhsT=W2k[:, cc, :], rhs=hid[:, 0, cc, 0:NCMP], start=(cc == 0), stop=(cc == 1)),
                            wait=[ev_hid, evP[bk]] if cc == 0 else (), sig=(cc == 1))
                    evP[bk] = ctx.op("act", lambda e, g=g, bk=bk: e.activation(
                        out=kcA[0:64, g, 0:NCMP], in_=P[bk][0:64, 0:NCMP], func=AF.Copy), wait=[evp], sig=True)
                    bk = pbank()
                    for cc in range(2):
                        evp = ctx.op("pe", lambda e, cc=cc, bk=bk: e.matmul(
                            P[bk][0:NCMP, 0:64], lhsT=hid[:, 1, cc, 0:NCMP], rhs=W2v[:, cc, :], start=(cc == 0), stop=(cc == 1)),
                            wait=[ev_hid, evP[bk]] if cc == 0 else (), sig=(cc == 1))
                    evP[bk] = ctx.op("act", lambda e, g=g, bk=bk: e.activation(
                        out=vcA[0:NCMP, g, 0:64], in_=P[bk][0:NCMP, 0:64], func=AF.Copy), wait=[evp], sig=True)
                ctx.barrier([evP[0], evP[1]])
            ctx.stack = st

            with contextlib.ExitStack() as st2:
                ctx.stack = st2
                qA = [sb(ctx, "qA%d" % i, [KA, SEQ], BF16) for i in range(4)]
                ksA = sb(ctx, "ksA", [KA, SEQ], BF16)
                kwA = sb(ctx, "kwA", [KA, SEQ], BF16)
                vsA = sb(ctx, "vsA", [128, NQB, 65], BF16)
                vwA = sb(ctx, "vwA", [128, NQB, 65], BF16)
                gat = sb(ctx, "gat", [128, NQB, 12], F32)
                pT = [sb(ctx, "pT%d" % i, [128, 128], BF16) for i in range(2)]
                tmpf = sb(ctx, "tmpf", [128, 128], F32)
                rz = sb(ctx, "rz", [128, 3, 4], F32)
                imp = sb(ctx, "imp", [128, 32], F32)
                wk = sb(ctx, "wk", [128, 32], F32)
                m8 = sb(ctx, "m8", [128, 16], F32)
                nm = sb(ctx, "nm", [128, 96], BF16)
                oacc = sb(ctx, "oacc", [128, 256], F32)
                oab = sb(ctx, "oab", [128, 256], BF16)
                s_c = ctx.sem("nsac")
                ctx.dma("pool", ksA[64:96, :], G["c_E"], s_c)
                ctx.dma("pool", ksA[96:97, :], G["c_ones_row"], s_c)
                ev_c = ctx.dma("pool", kwA[96:97, :], G["c_ones_row"], s_c)
                ctx.op("dve", lambda e: e.memset(kwA[64:96, :], 0.0))
                ctx.op("dve", lambda e: e.memset(nm[:, 0:64], 0.0))
                ctx.op("dve", lambda e: e.memset(vsA[:, :, 64:65], 1.0))
                ev_m1 = ctx.op("dve", lambda e: e.memset(vwA[:, :, 64:65], 1.0), sig=True)
                pti = [0]
                evPT = [None, None]
                ev_oacc_rd = None
                ev_oc_rd = None
                ev_os_rd = None
                ev_ow_rd = None
                ev_tp_rd = None
                for g in range(int(os.environ.get('NSA_DBG_G', 4))):
                    for hp in range(4):
                        ctx.dma("pool", qA[hp][96:97, :], G["c_qshift"][4 * g + hp:4 * g + hp + 1, :], s_c,
                                wait=[ev_ow_rd, ev_os_rd, ev_oc_rd])
                        ev_c = (s_c, s_c.v)
                        ev_m1 = ctx.op("dve", lambda e, hp=hp: e.memset(qA[hp][64:96, :], 0.0), sig=True)
                    for (dst, jj) in ((ksA, 2), (kwA, 4)):
                        col = D + jj * 256 + g * 64
                        for sl4 in range(4):
                            bk = pbank()
                            for k in range(8):
                                evp = ctx.op("pe", lambda e, k=k, bk=bk, col=col, sl4=sl4: e.matmul(
                                    P[bk][0:64, :], lhsT=Wn[:, k, col:col + 64], rhs=hT[:, k, sl4 * 512:(sl4 + 1) * 512],
                                    start=(k == 0), stop=(k == 7)), wait=[evP[bk]] if k == 0 else (), sig=(k == 7))
                            evP[bk] = ctx.op("act", lambda e, dst=dst, bk=bk, sl4=sl4: e.activation(
                                out=dst[0:64, sl4 * 512:(sl4 + 1) * 512], in_=P[bk][0:64, :], func=AF.Copy), wait=[evp], sig=True)
                    for hp in range(4):
                        col = (4 * g + hp) * 64
                        for sl4 in range(4):
                            bk = pbank()
                            for k in range(8):
                                evp = ctx.op("pe", lambda e, k=k, bk=bk, col=col, sl4=sl4: e.matmul(
                                    P[bk][0:64, :], lhsT=Wn[:, k, col:col + 64], rhs=hT[:, k, sl4 * 512:(sl4 + 1) * 512],
                                    start=(k == 0), stop=(k == 7)), wait=[evP[bk]] if k == 0 else (), sig=(k == 7))
                            evP[bk] = ctx.op("act", lambda e, hp=hp, bk=bk, sl4=sl4: e.activation(
                                out=qA[hp][0:64, sl4 * 512:(sl4 + 1) * 512], in_=P[bk][0:64, :], func=AF.Copy, scale=DH ** -0.5),
                                wait=[evp], sig=True)
                    cvs = D + 3 * 256 + g * 64
                    cvw = D + 5 * 256 + g * 64
                    cg = D + 6 * 256 + g * 12
                    for t in range(NQB):
                        bk = pbank()
                        for (c0, o0, n) in ((cvs, 0, 64), (cvw, 64, 64), (cg, 128, 12)):
                            for k in range(8):
                                evp = ctx.op("pe", lambda e, k=k, bk=bk, c0=c0, o0=o0, n=n, t=t: e.matmul(
                                    P[bk][:, o0:o0 + n], lhsT=hT[:, k, t * 128:(t + 1) * 128], rhs=Wn[:, k, c0:c0 + n],
                                    start=(k == 0), stop=(k == 7)), wait=[evP[bk]] if (k == 0 and o0 == 0) else (),
                                    sig=(k == 7 and o0 == 128))
                        ctx.op("act", lambda e, bk=bk, t=t: e.activation(out=vsA[:, t, 0:64], in_=P[bk][:, 0:64], func=AF.Copy),
                               wait=[evp, ev_ow_rd, ev_os_rd, ev_oc_rd])
                        ctx.op("act", lambda e, bk=bk, t=t: e.activation(out=vwA[:, t, 0:64], in_=P[bk][:, 64:128], func=AF.Copy))
                        evP[bk] = ctx.op("act", lambda e, bk=bk, t=t: e.activation(
                            out=gat[:, t, :], in_=P[bk][:, 128:140], func=AF.Sigmoid), sig=True)
                    ev_proj = evP[(pidx[0] - 1) % 2]

                    def score_tile(KT, kcols, hp, qb, bias_ap, mask_ap, clamp):
                        sb_ = sidx[0] % 2
                        sidx[0] += 1
                        nk = kcols[1] - kcols[0] if isinstance(kcols, tuple) else None
                        evs = ctx.op("pe", lambda e: e.matmul(
                            SPS[sb_][0:KT[1], :], lhsT=KT[0], rhs=qA[hp][:, qb * 128:(qb + 1) * 128], start=True, stop=True),
                            wait=[evS[sb_], ev_proj, ev_c, ev_m1], sig=True)
                        pi = pti[0] % 2
                        pti[0] += 1
                        nkk = KT[1]
                        if clamp:
                            evd = ctx.op("dve", lambda e: e.tensor_scalar(
                                tmpf[0:nkk, :], SPS[sb_][0:nkk, :], bias_ap, CLAMP, op0=ALU.add, op1=ALU.min), wait=[evs], sig=True)
                            evS[sb_] = evd
                            eva = ctx.op("act", lambda e: e.activation(out=tmpf[0:nkk, :], in_=tmpf[0:nkk, :], func=AF.Exp),
                                         wait=[evd], sig=True)
                            evp_ = ctx.op("dve", lambda e: e.tensor_tensor(pT[pi][0:nkk, :], tmpf[0:nkk, :], mask_ap, op=ALU.mult),
                                          wait=[eva, evPT[pi]], sig=True)
                        else:
                            eva = ctx.op("act", lambda e: e.activation(out=pT[pi][0:nkk, :], in_=SPS[sb_][0:nkk, :], func=AF.Exp,
                                                                      bias=bias_ap, scale=1.0), wait=[evs, evPT[pi]], sig=True)
                            evS[sb_] = eva
                            evp_ = eva
                            if mask_ap is not None:
                                evp_ = ctx.op("dve", lambda e: e.tensor_tensor(pT[pi][0:nkk, :], pT[pi][0:nkk, :], mask_ap, op=ALU.mult),
                                              wait=[eva], sig=True)
                        return pi, evp_

                    for qb in range(NQB):
                        for hp in range(4):
                            hd = 4 * g + hp
                            pi, evp_ = score_tile((kcA[:, g, 0:NCMP], NCMP), None, hp, qb, Bc[0:NCMP, hd:hd + 1],
                                                  cmask[0:NCMP, qb, :], True)
                            evPT[pi] = ctx.op("pe", lambda e, hp=hp, pi=pi: e.matmul(
                                OC[:, hp, :], lhsT=pT[pi][0:NCMP, :], rhs=vcA[0:NCMP, g, :], start=True, stop=True),
                                wait=[evp_, ev_oc_rd], sig=True)
                        ev_oc = evPT[pi]
                        ctx.op("dve", lambda e: e.tensor_scalar_max(rz[:, 0, :], OC[:, :, 96], 1e-30), wait=[ev_oc, ev_oacc_rd])
                        ctx.op("dve", lambda e: e.reciprocal(rz[:, 0, :], rz[:, 0, :]))
                        ctx.op("dve", lambda e: e.tensor_scalar(imp[:], OC[:, 0, 64:96], rz[:, 0, 0:1], None, op0=ALU.mult))
                        for hp in range(1, 4):
                            ctx.op("dve", lambda e, hp=hp: e.scalar_tensor_tensor(
                                out=imp[:], in0=OC[:, hp, 64:96], scalar=rz[:, 0, hp:hp + 1], in1=imp[:], op0=ALU.mult, op1=ALU.add))
                        ctx.op("dve", lambda e, qb=qb: e.tensor_tensor(rz[:, 0, :], rz[:, 0, :], gat[:, qb, 0:12:3], op=ALU.mult))
                        for hp in range(4):
                            ev_oc_rd = ctx.op("dve", lambda e, hp=hp: e.tensor_scalar(
                                oacc[:, hp * 64:(hp + 1) * 64], OC[:, hp, 0:64], rz[:, 0, hp:hp + 1], None, op0=ALU.mult), sig=(hp == 3))
                        ctx.op("dve", lambda e, qb=qb: e.tensor_tensor(imp[:], imp[:], topk[:, qb, 0, :], op=ALU.mult))
                        ctx.op("dve", lambda e, qb=qb: e.tensor_tensor(imp[:], imp[:], topk[:, qb, 1, :], op=ALU.add))
                        ev_i = ctx.op("dve", lambda e, qb=qb: e.tensor_copy(imp[:], imp[:]), sig=True)
                        ev_a1 = ctx.op("dve", lambda e: e.max(out=m8[:, 0:8], in_=imp[:]), wait=[ev_i], sig=True)
                        ev_a2 = ctx.op("dve", lambda e: e.match_replace(out=wk[:], in_to_replace=m8[:, 0:8], in_values=imp[:], imm_value=-2.0),
                                       wait=[ev_a1], sig=True)
                        ev_a3 = ctx.op("dve", lambda e: e.max(out=m8[:, 8:16], in_=wk[:]), wait=[ev_a2], sig=True)
                        ctx.op("dve", lambda e: e.tensor_scalar(wk[:], imp[:], m8[:, 15:16], None, op0=ALU.is_ge), wait=[ev_a3])
                        ev_nm = ctx.op("dve", lambda e: e.tensor_scalar(nm[:, 64:96], wk[:], -1.0, NEGM, op0=ALU.add, op1=ALU.mult),
                                       wait=[ev_tp_rd], sig=True)
                        for hp in range(4):
                            hd = 4 * g + hp
                            k0 = max(0, qb - 4)
                            for kt in range(k0, qb + 1):
                                diag = (kt == qb)
                                far = (kt == qb - 4)
                                pi, evp_ = score_tile((kwA[:, kt * 128:(kt + 1) * 128], 128), None, hp, qb, Bk[:, hd, kt:kt + 1],
                                                      tri[:] if diag else (triU[:] if far else None), diag)
                                evPT[pi] = ctx.op("pe", lambda e, hp=hp, pi=pi, kt=kt, qb=qb, k0=k0: e.matmul(
                                    OW[:, hp, :], lhsT=pT[pi][:], rhs=vwA[:, kt, :], start=(kt == k0), stop=(kt == qb)),
                                    wait=[evp_, ev_ow_rd], sig=True)
                        ev_ow = evPT[pi]
                        evt = ctx.op("pe", lambda e: e.transpose(TP[0:96, 0, :], nm[:], G["ident"][:]), wait=[ev_nm, ev_tp_rd], sig=True)
                        for hp in range(4):
                            ev_m1 = ctx.op("act", lambda e, hp=hp, qb=qb: e.activation(
                                out=qA[hp][64:96, qb * 128:(qb + 1) * 128], in_=TP[64:96, 0, :], func=AF.Copy), wait=[evt], sig=True)
                        ev_tp_rd = ev_m1
                        for hp in range(4):
                            hd = 4 * g + hp
                            for kt in range(qb + 1):
                                diag = (kt == qb)
                                pi, evp_ = score_tile((ksA[:, kt * 128:(kt + 1) * 128], 128), None, hp, qb, Bk[:, hd, kt:kt + 1],
                                                      tri[:] if diag else None, diag)
                                evPT[pi] = ctx.op("pe", lambda e, hp=hp, pi=pi, kt=kt, qb=qb: e.matmul(
                                    OS[:, hp, :], lhsT=pT[pi][:], rhs=vsA[:, kt, :], start=(kt == 0), stop=(kt == qb)),
                                    wait=[evp_, ev_os_rd], sig=True)
                        ev_os = evPT[pi]
                        for (bi, OX, ev_ox) in ((1, OS, ev_os), (2, OW, ev_ow)):
                            ctx.op("dve", lambda e, bi=bi, OX=OX: e.tensor_scalar_max(rz[:, bi, :], OX[:, :, 64], 1e-30), wait=[ev_ox])
                            ctx.op("dve", lambda e, bi=bi: e.reciprocal(rz[:, bi, :], rz[:, bi, :]))
                            ctx.op("dve", lambda e, bi=bi, qb=qb: e.tensor_tensor(rz[:, bi, :], rz[:, bi, :], gat[:, qb, bi:12:3], op=ALU.mult))
                            for hp in range(4):
                                evr = ctx.op("dve", lambda e, hp=hp, bi=bi, OX=OX: e.scalar_tensor_tensor(
                                    out=oacc[:, hp * 64:(hp + 1) * 64], in0=OX[:, hp, 0:64], scalar=rz[:, bi, hp:hp + 1],
                                    in1=oacc[:, hp * 64:(hp + 1) * 64], op0=ALU.mult, op1=ALU.add), sig=(hp == 3))
                            if bi == 1:
                                ev_os_rd = evr
                            else:
                                ev_ow_rd = evr
                        ev_ob = ctx.op("dve", lambda e: e.tensor_copy(oab[:], oacc[:]), wait=[ev_tp_rd], sig=True)
                        ev_oacc_rd = ev_ob
                        for hf in range(2):
                            evt = ctx.op("pe", lambda e, hf=hf: e.transpose(TP[:, hf, :], oab[:, hf * 128:(hf + 1) * 128], G["ident"][:]),
                                         wait=[ev_ob, ev_tp_rd], sig=True)
                        ev_tp_rd = ctx.op("act", lambda e, g=g, qb=qb: e.activation(
                            out=oT[:, 2 * g:2 * g + 2, qb * 128:(qb + 1) * 128], in_=TP[:], func=AF.Copy), wait=[evt], sig=True)
                evd = None
                if "dbg" in G and b == 0:
                    evd = dbg_dump(ctx, G, [("oT0", oT[:, 0, :], SEQ), ("oT1", oT[:, 1, :], SEQ), ("rz", rz[:, :, :], 12),
                                            ("gat15", gat[:, 15, :], 12), ("imp", imp[:, :], 32), ("wk", wk[:, :], 32),
                                            ("m8", m8[:, :], 16), ("oacc", oacc[:, :], 256)],
                                   wait=[ev_tp_rd, ev_os_rd, ev_ow_rd])
                ctx.barrier([ev_tp_rd, evPT[0], evPT[1], ev_os_rd, ev_ow_rd, evd])
            ctx.stack = st

            for t in range(NQB):
                tg = b * NQB + t
                epi_load(ctx, G, E, tg)
                for m in range(8):
                    for f in range(8):
                        evy = ctx.op("pe", lambda e, m=m, f=f, t=t: e.matmul(
                            E.Y[m % 2][:], lhsT=Wno[:, f, m * 128:(m + 1) * 128], rhs=oT[:, f, t * 128:(t + 1) * 128],
                            start=(f == 0), stop=(f == 7)),
                            wait=[ev_tp_rd, E.ev_ybank[m % 2], evP[m % 2]] if f == 0 else (), sig=(f == 7))
                    epi_z(ctx, G, E, tg, L, s, m, evy)
                    evP[m % 2] = E.ev_ybank[m % 2]
                ctx.op("pe", None, wait=[evS[0], evS[1]])
                ev_st = epi_stats_pe(ctx, G, E, tg)
                last_store = epi_norm(ctx, G, E, tg, L, s, ev_st)
                evS[0] = E.ev_stats_rd
                evS[1] = E.ev_stats_rd
            ctx.barrier([last_store, E.ev_xfree[0], E.ev_xfree[1], ev_st])
    ctx.stack = outer

def build_program(phases):
    nc = bass.Bass("TRN2", target_bir_lowering=False)
    ins = {}

    def inp(name, shape):
        ins[name] = nc.dram_tensor(name, list(shape), F32, kind="ExternalInput").ap()
        return ins[name]

    xT_in = inp("xT_in", [D, TOK])
    cT = inp("cT", [128, 8, NB])
    ada_w = inp("ada_w", [DEPTH, D, 9 * D])
    ada_b = inp("ada_bT", [128, DEPTH, 72])
    lng = inp("lng", [128, DEPTH, 3, 8])
    lnb = inp("lnb", [128, DEPTH, 3, 8])
    need_ffn = any(k == "ffn" for (k, L_, s_) in phases)
    need_ret = any(k == "mix" and L_ % 2 == 0 for (k, L_, s_) in phases)
    need_nsa = any(k == "mix" and L_ % 2 == 1 for (k, L_, s_) in phases)
    if need_ffn:
        ffn_w_in = inp("ffn_w_in", [DEPTH, 2, D, 2 * DFF])
        ffn_w_out = inp("ffn_w_out", [DEPTH, 2, DFF, D])
    if need_ret:
        ret_w_in = inp("ret_w_in", [2, D, 6144])
        ret_w_out = inp("ret_w_out", [2, 2048, D])
    if need_nsa:
        NW = {}
        NW["w_in"] = inp("nsa_w_in", [2, D, 2608])
        NW["w_out"] = inp("nsa_w_out", [2, D, D])
        NW["ck_w1"] = inp("nsa_ck_w1", [2, 2048, 256])
        NW["ck_w2"] = inp("nsa_ck_w2", [2, 256, 64])
        NW["cv_w1"] = inp("nsa_cv_w1", [2, 2048, 256])
        NW["cv_w2"] = inp("nsa_cv_w2", [2, 256, 64])
        NW["pekT"] = inp("pekT", [2, 64, 32])
        NW["pevT"] = inp("pevT", [2, 64, 32])
        NT_ = {}
        for (nm_, shp) in (("c_E", [32, SEQ]), ("c_ones_row", [1, SEQ]), ("c_ones4", [1, 4, 128]), ("c_qshift", [16, SEQ]),
                           ("c_Bk", [128, 16, 16]), ("c_Bc", [128, 16]), ("c_cmask", [128, 16, 128]), ("c_tri", [128, 128]),
                           ("c_triU", [128, 128]), ("c_topk", [128, 16, 2, 32]), ("c_ovl", [128, 32])):
            NT_[nm_] = inp(nm_, shp)
    c_dq = inp("c_dq", [128, 4, 128])
    c_dk = inp("c_dk", [128, 4])
    c_DT = inp("c_DT", [128, 4, 128])
    c_ident = inp("c_ident", [128, 128])
    out = nc.dram_tensor("out", [D, TOK], F32, kind="ExternalOutput").ap()
    dbg = nc.dram_tensor("dbg", [128, 16384], F32, kind="ExternalOutput").ap() if os.environ.get("DBG") else None

    import contextlib
    with contextlib.ExitStack() as stack:
        ctx = Ctx(nc, stack)
        G = {}
        G["xT"] = out
        if dbg is not None:
            G["dbg"] = dbg
            G["dbg_layout"] = DBG_LAYOUT
            del DBG_LAYOUT[:]
        G["cT"] = sb(ctx, "cT_sb", [128, 8, NB], F32)
        G["modT"] = sb(ctx, "modT", [128, DEPTH, 72, NB], F32)
        G["A"] = sb(ctx, "Atab", [128, DEPTH, 3, NB, 8], F32)
        G["C"] = sb(ctx, "Ctab", [128, DEPTH, 3, NB, 8], F32)
        G["lng"] = sb(ctx, "lng_sb", [128, DEPTH, 3, 8], F32)
        G["lnb"] = sb(ctx, "lnb_sb", [128, DEPTH, 3, 8], F32)
        G["ones_bf"] = sb(ctx, "ones_bf", [128, 128], BF16)
        G["eps"] = sb(ctx, "eps_sb", [128, 1], F32)
        G["dqT"] = sb(ctx, "dqT_sb", [128, 4, 128], F32)
        G["dk"] = sb(ctx, "dk_sb", [128, 4], F32)
        G["DT"] = sb(ctx, "DT_sb", [128, 4, 128], F32)
        G["ident"] = sb(ctx, "ident_sb", [128, 128], BF16)
        G["gneps"] = sb(ctx, "gneps_sb", [128, 1], F32)
        s0 = ctx.sem("init")
        ctx.dma("sp", G["dqT"][:], c_dq, s0)
        ctx.dma("sp", G["dk"][:], c_dk, s0)
        ctx.dma("sp", G["DT"][:], c_DT, s0)
        s0p = ctx.sem("initp")
        ev0p = ctx.dma("pool", G["ident"][:], c_ident, s0p)
        ctx.op("dve", lambda e: e.memset(G["gneps"][:], 1e-6))
        ctx.dma("sp", G["lng"][:], lng, s0)
        ev0 = ctx.dma("sp", G["lnb"][:], lnb, s0)
        s1 = ctx.sem("xcopy")
        for m in range(8):
            ev1 = ctx.dma("act" if m % 2 else "sp", out[m * 128:(m + 1) * 128, :], xT_in[m * 128:(m + 1) * 128, :], s1)
        ctx.op("dve", lambda e: e.memset(G["eps"][:], EPS_P))
        ev2 = ctx.op("dve", lambda e: e.memset(G["ones_bf"][:], 1.0 / D), sig=True)
        ctx.barrier([ev0, ev1, ev2, ev0p])
        if not os.environ.get('SKIP_ADA'):
            ada_phase(ctx, G, cT, ada_w, ada_b)
        for ph in phases:
            kind, L, s = ph
            if kind == "ffn":
                ffn_phase(ctx, G, L, s, ffn_w_in[L, s // 2], ffn_w_out[L, s // 2])
            elif kind == "mix" and L % 2 == 0:
                ret_phase(ctx, G, L, ret_w_in[L // 2], ret_w_out[L // 2])
            elif kind == "mix" and L % 2 == 1:
                G.update(NT_)
                nsa_phase(ctx, G, L, {k_: v_[L // 2] for k_, v_ in NW.items()})
        ctx.emit()
    return nc, set(ins.keys())


def host_inputs(inputs, core):
    b0 = core * NB
    x = np.asarray(inputs["x"][b0:b0 + NB], np.float32).reshape(TOK, D)
    m = {}
    m["xT_in"] = np.ascontiguousarray(x.T)
    c = np.asarray(inputs["c"][b0:b0 + NB], np.float32)
    m["cT"] = np.ascontiguousarray(c.reshape(NB, 8, 128).transpose(2, 1, 0))
    m["ada_w"] = np.asarray(inputs["ada_w"], np.float32)
    m["ada_bT"] = np.ascontiguousarray(np.asarray(inputs["ada_b"], np.float32).reshape(DEPTH, 72, 128).transpose(2, 0, 1))
    m["lng"] = np.ascontiguousarray(np.asarray(inputs["ln_g"], np.float32).reshape(DEPTH, 3, 8, 128).transpose(3, 0, 1, 2))
    m["lnb"] = np.ascontiguousarray(np.asarray(inputs["ln_b"], np.float32).reshape(DEPTH, 3, 8, 128).transpose(3, 0, 1, 2))
    m["ffn_w_in"] = np.asarray(inputs["ffn_w_in"], np.float32)
    m["ffn_w_out"] = np.asarray(inputs["ffn_w_out"], np.float32)
    m["ret_w_in"] = np.asarray(inputs["ret_w_in"], np.float32)
    m["ret_w_out"] = np.asarray(inputs["ret_w_out"], np.float32)
    gam = np.array(GAM, np.float64)
    i = np.arange(128, dtype=np.float64)
    dq = gam[:, None] ** (i[None, :] + 1.0)
    m["c_dq"] = np.ascontiguousarray(np.broadcast_to(dq[None], (128, 4, 128))).astype(np.float32)
    m["c_dk"] = np.ascontiguousarray((gam[None, :] ** (127.0 - i[:, None])) * (256.0 ** -0.5)).astype(np.float32)
    diff = i[None, :] - i[:, None]
    DT = np.where(diff[None] >= 0, gam[:, None, None] ** np.maximum(diff[None], 0.0), 0.0) * (256.0 ** -0.5)
    m["c_DT"] = np.ascontiguousarray(DT.transpose(1, 0, 2)).astype(np.float32)
    m["c_ident"] = np.eye(128, dtype=np.float32)
    for k_ in ("nsa_w_in", "nsa_w_out", "nsa_ck_w1", "nsa_ck_w2", "nsa_cv_w1", "nsa_cv_w2"):
        m[k_] = np.asarray(inputs[k_], np.float32)
    m["pekT"] = np.ascontiguousarray(np.asarray(inputs["nsa_pe_k"], np.float32).transpose(0, 2, 1))
    m["pevT"] = np.ascontiguousarray(np.asarray(inputs["nsa_pe_v"], np.float32).transpose(0, 2, 1))
    m.update(nsa_tables())
    return m


_NSA_TABLES = {}


def nsa_tables():
    if _NSA_TABLES:
        return _NSA_TABLES
    T = _NSA_TABLES
    key = np.arange(SEQ)
    T["c_E"] = (key[None, :] // 64 == np.arange(32)[:, None]).astype(np.float32)
    T["c_ones_row"] = np.ones((1, SEQ), np.float32)
    T["c_ones4"] = np.ones((1, 4, 128), np.float32)
    sl = np.array(SLOPES, np.float64)
    T["c_qshift"] = (-sl[:, None] * key[None, :].astype(np.float64)).astype(np.float32)
    j = np.arange(128, dtype=np.float64)
    kt = np.arange(16, dtype=np.float64)
    T["c_Bk"] = (sl[None, :, None] * (kt[None, None, :] * 128.0 + j[:, None, None])).astype(np.float32)
    n = np.arange(128)
    cend = 16.0 * n + 31.0
    T["c_Bc"] = (sl[None, :] * cend[:, None]).astype(np.float32)
    t = (np.arange(16)[:, None] * 128 + np.arange(128)[None, :])
    cm = (t[None, :, :] >= cend[:, None, None]) & (n[:, None, None] < NCMP)
    T["c_cmask"] = cm.astype(np.float32)
    T["c_tri"] = (np.arange(128)[None, :] >= np.arange(128)[:, None]).astype(np.float32)
    T["c_triU"] = (np.arange(128)[:, None] > np.arange(128)[None, :]).astype(np.float32)
    tq = (np.arange(16)[None, :] * 128 + np.arange(128)[:, None])
    blk = np.arange(32)
    valid = blk[None, None, :] * 64 <= tq[:, :, None]
    cur = tq[:, :, None] // 64
    forced = (blk[None, None, :] == 0) | (blk[None, None, :] == cur) | (blk[None, None, :] == cur - 1)
    tk = np.zeros((128, 16, 2, 32), np.float32)
    tk[:, :, 0, :] = (valid & ~forced)
    tk[:, :, 1, :] = np.where(valid, np.where(forced, 1e4 + blk[None, None, :], 0.0), -1.0)
    T["c_topk"] = tk
    c0 = np.arange(NCMP) * 16
    s0 = np.arange(32) * 64
    lo = np.maximum(c0[:, None], s0[None, :])
    hi = np.minimum(c0[:, None] + 32, s0[None, :] + 64)
    ov = np.zeros((128, 32), np.float32)
    ov[:NCMP] = np.clip(hi - lo, 0, None) / 32.0
    T["c_ovl"] = ov
    return T


def all_phases():
    ph = []
    for L in range(DEPTH):
        ph.append(("ffn", L, 0))
        ph.append(("mix", L, 1))
        ph.append(("ffn", L, 2))
    return ph


def run(inputs, phases, cores=8):
    nc, names = build_program(phases)
    in_maps = [{k: v for k, v in host_inputs(inputs, c).items() if k in names} for c in range(cores)]
    res = run_bass_kernel_spmd(nc, in_maps, core_ids=list(range(cores)))
    outs = [np.asarray(r["out"]).T.reshape(NB, SEQ, D) for r in res.results]
    return np.concatenate(outs, axis=0)


def kernel(**inputs):
    return run(inputs, all_phases(), 8).astype(np.float32)
```
